# Optimizing a Trainium2 kernel written in Bass

```python
import math
import jax, jax.numpy as jnp
from jax import lax
import numpy as np


D_MODEL = 1024
BATCH = 4
SEQ = 8192
DEPTH = 2

HEAD_DIM = 64
GROUP_HEADS = 4
GROUP_WIDTH = GROUP_HEADS * HEAD_DIM
N_MIXERS = 4
D_MIX = N_MIXERS * GROUP_WIDTH
D_FF = 4 * D_MODEL
NORM_EPS = 1e-6

NUM_BUCKETS = 32
MAX_DISTANCE = 2048
N_BIAS_HEADS = 2 * GROUP_HEADS

DILATED_PATTERNS = ((128, 1), (512, 4), (2048, 16))
DIL_BLOCK = 128

LRU_WIDTH = GROUP_WIDTH
LRU_BLOCKS = GROUP_HEADS
LRU_BLOCK_DIM = LRU_WIDTH // LRU_BLOCKS
LRU_C = 8.0
CONV_WIDTH = 4

SSM_HEADS = GROUP_HEADS
SSM_HEAD_DIM = HEAD_DIM
SSM_INNER = SSM_HEADS * SSM_HEAD_DIM
SSM_GROUPS = 2
SSM_STATE = 128
SSM_CHUNK = 128
SSM_CONV_DIM = SSM_INNER + 2 * SSM_GROUPS * SSM_STATE

DIFF_HEADS = GROUP_HEADS
DIFF_V_DIM = HEAD_DIM
DIFF_QK_DIM = HEAD_DIM // 2
DIFF_BLOCK = 128

A_COLS = 3 * GROUP_WIDTH
B_COLS = 2 * LRU_WIDTH
C_COLS = SSM_INNER + SSM_CONV_DIM + SSM_HEADS
D_COLS = 3 * GROUP_WIDTH
P_IN = A_COLS + B_COLS + C_COLS + D_COLS

kernel_name = 'hybrid_parallel_heads_dilated_lru_ssd_diff'


def rms_norm(x, g, eps=NORM_EPS):
    xf = x.astype(jnp.float32)
    y = xf * lax.rsqrt(jnp.mean(xf * xf, axis=-1, keepdims=True) + eps)
    return (y * g.astype(jnp.float32)).astype(x.dtype)


def t5_bucket(dist):
    n = jnp.maximum(dist, 0)
    max_exact = NUM_BUCKETS // 2
    nf = jnp.maximum(n, 1).astype(jnp.float32)
    large = max_exact + (jnp.log(nf / max_exact) / math.log(MAX_DISTANCE / max_exact)
                         * (NUM_BUCKETS - max_exact)).astype(jnp.int32)
    large = jnp.minimum(large, NUM_BUCKETS - 1)
    return jnp.where(n < max_exact, n, large)


def causal_depthwise_conv(x, w, b):
    k = w.shape[0]
    y = lax.conv_general_dilated(x, w[:, None, :].astype(x.dtype), window_strides=(1,),
                                 padding=((k - 1, 0),), dimension_numbers=('NWC', 'WIO', 'NWC'),
                                 feature_group_count=x.shape[-1])
    return y + b.astype(x.dtype)


def dilated_window_attention(q, k, v, bias_table):
    bsz, s, h, dh = q.shape
    scale = dh ** -0.5
    qi = jnp.arange(DIL_BLOCK)[:, None]
    kj = jnp.arange(2 * DIL_BLOCK)[None, :]
    rel = qi + DIL_BLOCK - kj
    outs, lses = [], []
    for window, dil in DILATED_PATTERNS:
        span = window // dil
        L = s // dil
        nb = -(-L // DIL_BLOCK)
        Lp = nb * DIL_BLOCK

        def strided(t):
            t = t.reshape(bsz, L, dil, h, dh).transpose(0, 3, 2, 1, 4)
            t = jnp.pad(t, ((0, 0), (0, 0), (0, 0), (0, Lp - L), (0, 0)))
            return t.reshape(bsz, h, dil, nb, DIL_BLOCK, dh)

        def with_prev(t):
            prev = jnp.pad(t, ((0, 0), (0, 0), (0, 0), (1, 0), (0, 0), (0, 0)))[:, :, :, :-1]
            return jnp.concatenate([prev, t], axis=-2)

        qb = strided(q)
        kc = with_prev(strided(k))
        vc = with_prev(strided(v)).astype(jnp.float32)
        blk = jnp.arange(nb)[:, None, None]
        valid = (rel >= 0) & (rel <= span) & (blk * DIL_BLOCK + kj - DIL_BLOCK >= 0)
        bias = bias_table[t5_bucket(rel * dil)].astype(jnp.float32)
        bias = jnp.moveaxis(bias, -1, 0)[:, None, None]
        sc = jnp.einsum('bhrnqd,bhrnkd->bhrnqk', qb, kc).astype(jnp.float32) * scale + bias
        sc = jnp.where(valid, sc, -jnp.inf)
        m = jnp.max(sc, axis=-1, keepdims=True)
        e = jnp.exp(sc - m)
        den = jnp.sum(e, axis=-1, keepdims=True)
        o = jnp.einsum('bhrnqk,bhrnkd->bhrnqd', e, vc) / den
        lse = (m + jnp.log(den))[..., 0]
        o = o.reshape(bsz, h, dil, Lp, dh)[:, :, :, :L].transpose(0, 1, 3, 2, 4).reshape(bsz, h, s, dh)
        lse = lse.reshape(bsz, h, dil, Lp)[..., :L].transpose(0, 1, 3, 2).reshape(bsz, h, s)
        outs.append(o)
        lses.append(lse)
    wts = jax.nn.softmax(jnp.stack(lses), axis=0)
    out = jnp.sum(wts[..., None] * jnp.stack(outs), axis=0)
    return out.transpose(0, 2, 1, 3).reshape(bsz, s, h * dh).astype(q.dtype)


def rg_lru(xg, xr, conv_w, conv_b, wa, ba, wx, bx, lam):
    bsz, s, _ = xr.shape
    f32 = jnp.float32
    xc = causal_depthwise_conv(xr, conv_w, conv_b).astype(f32)
    xh = xc.reshape(bsz, s, LRU_BLOCKS, LRU_BLOCK_DIM)
    r = jax.nn.sigmoid(jnp.einsum('bshi,hij->bshj', xh, wa.astype(f32)).reshape(bsz, s, LRU_WIDTH) + ba)
    i = jax.nn.sigmoid(jnp.einsum('bshi,hij->bshj', xh, wx.astype(f32)).reshape(bsz, s, LRU_WIDTH) + bx)
    log_a = -LRU_C * r * jax.nn.softplus(-lam.astype(f32))
    a = jnp.exp(log_a)
    b = jnp.sqrt(-jnp.expm1(2.0 * log_a)) * (i * xc)

    def combine(left, right):
        return (left[0] * right[0], right[0] * left[1] + right[1])

    _, hs = lax.associative_scan(combine, (a, b), axis=1)
    return (jax.nn.gelu(xg.astype(f32), approximate=True) * hs).astype(xr.dtype)


def segsum(a):
    t = a.shape[-1]
    cs = jnp.cumsum(a, axis=-1)
    diff = cs[..., :, None] - cs[..., None, :]
    mask = jnp.tril(jnp.ones((t, t), dtype=bool))
    return jnp.where(mask, diff, -jnp.inf)


def mamba2_ssd(z, xbc, dt, conv_w, conv_b, dt_bias, a_log, d_skip, norm_gain):
    bsz, s, _ = z.shape
    f32 = jnp.float32
    xbc = jax.nn.silu(causal_depthwise_conv(xbc, conv_w, conv_b).astype(f32))
    xs, bm, cm = jnp.split(xbc, [SSM_INNER, SSM_INNER + SSM_GROUPS * SSM_STATE], axis=-1)
    dt = jax.nn.softplus(dt.astype(f32) + dt_bias)
    a = -jnp.exp(a_log.astype(f32))
    nc = s // SSM_CHUNK
    rh = SSM_HEADS // SSM_GROUPS
    xh = xs.reshape(bsz, nc, SSM_CHUNK, SSM_GROUPS, rh, SSM_HEAD_DIM)
    dth = dt.reshape(bsz, nc, SSM_CHUNK, SSM_GROUPS, rh)
    xdt = xh * dth[..., None]
    bm = bm.reshape(bsz, nc, SSM_CHUNK, SSM_GROUPS, SSM_STATE)
    cm = cm.reshape(bsz, nc, SSM_CHUNK, SSM_GROUPS, SSM_STATE)
    adt = (dth * a.reshape(SSM_GROUPS, rh)).transpose(0, 3, 4, 1, 2)
    a_cum = jnp.cumsum(adt, axis=-1)
    lmat = jnp.exp(segsum(adt))
    cb = jnp.einsum('bclgn,bcsgn->bcgls', cm, bm)
    y_diag = jnp.einsum('bcgls,bgrcls,bcsgrp->bclgrp', cb, lmat, xdt)
    decay_states = jnp.exp(a_cum[..., -1:] - a_cum)
    states = jnp.einsum('bcsgn,bgrcs,bcsgrp->bcgrpn', bm, decay_states, xdt)
    chunk_decay = jnp.exp(a_cum[..., -1])

    def step(carry, inp):
        st, dec = inp
        return carry * dec[..., None, None] + st, carry

    init = jnp.zeros((bsz, SSM_GROUPS, rh, SSM_HEAD_DIM, SSM_STATE), f32)
    _, prev = lax.scan(step, init, (jnp.moveaxis(states, 1, 0), jnp.moveaxis(chunk_decay, -1, 0)))
    prev = jnp.moveaxis(prev, 0, 1)
    y_off = jnp.einsum('bclgn,bcgrpn,bgrcl->bclgrp', cm, prev, jnp.exp(a_cum))
    y = y_diag + y_off + xh * d_skip.astype(f32).reshape(SSM_GROUPS, rh)[..., None]
    y = y.reshape(bsz, s, SSM_INNER) * jax.nn.silu(z.astype(f32))
    y = rms_norm(y.reshape(bsz, s, SSM_GROUPS, SSM_INNER // SSM_GROUPS),
                 norm_gain.reshape(SSM_GROUPS, SSM_INNER // SSM_GROUPS))
    return y.reshape(bsz, s, SSM_INNER).astype(z.dtype)


def diff_attention(q, k, v, lam, lam_init, bias_table, sub_gain):
    bsz, s, h = q.shape[:3]
    scale = DIFF_QK_DIM ** -0.5
    nb = s // DIFF_BLOCK
    qb = q.reshape(bsz, nb, DIFF_BLOCK, h, 2, DIFF_QK_DIM).transpose(1, 0, 3, 4, 2, 5)
    kt = k.transpose(0, 2, 3, 1, 4)
    vt = v.transpose(0, 2, 1, 3).astype(jnp.float32)
    kpos = jnp.arange(s)

    def one_block(args):
        qblk, bi = args
        qpos = bi * DIFF_BLOCK + jnp.arange(DIFF_BLOCK)
        dist = qpos[:, None] - kpos[None, :]
        bias = jnp.moveaxis(bias_table[t5_bucket(dist)].astype(jnp.float32), -1, 0)
        sc = jnp.einsum('bhcqd,bhckd->bhcqk', qblk, kt).astype(jnp.float32) * scale + bias[:, None]
        sc = jnp.where(dist >= 0, sc, -jnp.inf)
        p = jax.nn.softmax(sc, axis=-1)
        attn = p[:, :, 0] - lam * p[:, :, 1]
        return jnp.einsum('bhqk,bhkd->bhqd', attn, vt)

    o = lax.map(one_block, (qb, jnp.arange(nb)))
    o = o.transpose(1, 0, 3, 2, 4).reshape(bsz, s, h, DIFF_V_DIM)
    o = rms_norm(o, sub_gain) * (1.0 - lam_init)
    return o.reshape(bsz, s, h * DIFF_V_DIM).astype(q.dtype)


def setup_inputs(seed: int = 0) -> dict:
    key = jax.random.key(seed)
    ks = jax.random.split(key, 32)
    f32 = jnp.float32

    def nrm(k, shape, scale):
        return jax.random.normal(k, shape, f32) * scale

    def gain(k, shape):
        return 1.0 + 0.05 * jax.random.normal(k, shape, f32)

    u = jax.random.uniform(ks[12], (DEPTH, LRU_WIDTH), f32, 0.9, 0.999)
    a_base = u ** (1.0 / LRU_C)
    lru_lambda = jnp.log(a_base) - jnp.log1p(-a_base)
    dt0 = jnp.exp(jax.random.uniform(ks[15], (DEPTH, SSM_HEADS), f32, math.log(1e-3), math.log(1e-1)))
    ssm_dt_bias = dt0 + jnp.log(-jnp.expm1(-dt0))
    return {
        'x': nrm(ks[0], (BATCH, SEQ, D_MODEL), 1.0),
        'rel_bias': nrm(ks[1], (NUM_BUCKETS, N_BIAS_HEADS), 0.5),
        'norm_mix_pre': gain(ks[2], (DEPTH, D_MODEL)),
        'norm_mix_post': gain(ks[3], (DEPTH, D_MODEL)),
        'norm_ffn_pre': gain(ks[4], (DEPTH, D_MODEL)),
        'norm_ffn_post': gain(ks[5], (DEPTH, D_MODEL)),
        'w_in': nrm(ks[6], (DEPTH, D_MODEL, P_IN), D_MODEL ** -0.5),
        'w_out': nrm(ks[7], (DEPTH, D_MIX, D_MODEL), D_MIX ** -0.5),
        'lru_conv_w': nrm(ks[8], (DEPTH, CONV_WIDTH, LRU_WIDTH), CONV_WIDTH ** -0.5),
        'lru_conv_b': nrm(ks[9], (DEPTH, LRU_WIDTH), 0.02),
        'lru_wa': nrm(ks[10], (DEPTH, LRU_BLOCKS, LRU_BLOCK_DIM, LRU_BLOCK_DIM), LRU_BLOCK_DIM ** -0.5),
        'lru_ba': nrm(ks[11], (DEPTH, LRU_WIDTH), 0.02),
        'lru_wx': nrm(ks[13], (DEPTH, LRU_BLOCKS, LRU_BLOCK_DIM, LRU_BLOCK_DIM), LRU_BLOCK_DIM ** -0.5),
        'lru_bx': nrm(ks[14], (DEPTH, LRU_WIDTH), 0.02),
        'lru_lambda': lru_lambda,
        'ssm_conv_w': nrm(ks[16], (DEPTH, CONV_WIDTH, SSM_CONV_DIM), CONV_WIDTH ** -0.5),
        'ssm_conv_b': nrm(ks[17], (DEPTH, SSM_CONV_DIM), 0.02),
        'ssm_dt_bias': ssm_dt_bias,
        'ssm_a_log': jnp.log(jax.random.uniform(ks[18], (DEPTH, SSM_HEADS), f32, 1.0, 16.0)),
        'ssm_d': 1.0 + 0.1 * jax.random.normal(ks[19], (DEPTH, SSM_HEADS), f32),
        'ssm_norm': gain(ks[20], (DEPTH, SSM_INNER)),
        'diff_lq1': nrm(ks[21], (DEPTH, DIFF_QK_DIM), 0.1),
        'diff_lk1': nrm(ks[22], (DEPTH, DIFF_QK_DIM), 0.1),
        'diff_lq2': nrm(ks[23], (DEPTH, DIFF_QK_DIM), 0.1),
        'diff_lk2': nrm(ks[24], (DEPTH, DIFF_QK_DIM), 0.1),
        'diff_norm': gain(ks[25], (DEPTH, DIFF_V_DIM)),
        'w_ff_up': nrm(ks[26], (DEPTH, D_MODEL, D_FF), D_MODEL ** -0.5),
        'w_ff_down': nrm(ks[27], (DEPTH, D_FF, D_MODEL), D_FF ** -0.5),
    }


def reference(x, rel_bias, norm_mix_pre, norm_mix_post, norm_ffn_pre, norm_ffn_post, w_in, w_out,
              lru_conv_w, lru_conv_b, lru_wa, lru_ba, lru_wx, lru_bx, lru_lambda,
              ssm_conv_w, ssm_conv_b, ssm_dt_bias, ssm_a_log, ssm_d, ssm_norm,
              diff_lq1, diff_lk1, diff_lq2, diff_lk2, diff_norm, w_ff_up, w_ff_down):
    bsz, s, _ = x.shape
    h = x
    for layer in range(DEPTH):
        u = rms_norm(h, norm_mix_pre[layer])
        proj = jnp.einsum('bsd,dp->bsp', u, w_in[layer])
        pa, pb, pc, pd = jnp.split(proj, [A_COLS, A_COLS + B_COLS, A_COLS + B_COLS + C_COLS], axis=-1)

        qa, ka, va = (t.reshape(bsz, s, GROUP_HEADS, HEAD_DIM) for t in jnp.split(pa, 3, axis=-1))
        ya = dilated_window_attention(qa, ka, va, rel_bias[:, :GROUP_HEADS])

        gb, xb = jnp.split(pb, 2, axis=-1)
        yb = rg_lru(gb, xb, lru_conv_w[layer], lru_conv_b[layer], lru_wa[layer], lru_ba[layer],
                    lru_wx[layer], lru_bx[layer], lru_lambda[layer])

        zc, xbc, dtc = jnp.split(pc, [SSM_INNER, SSM_INNER + SSM_CONV_DIM], axis=-1)
        yc = mamba2_ssd(zc, xbc, dtc, ssm_conv_w[layer], ssm_conv_b[layer], ssm_dt_bias[layer],
                        ssm_a_log[layer], ssm_d[layer], ssm_norm[layer])

        qd, kd, vd = jnp.split(pd, 3, axis=-1)
        lam_init = 0.8 - 0.6 * math.exp(-0.3 * layer)
        lam = (jnp.exp(jnp.sum(diff_lq1[layer].astype(jnp.float32) * diff_lk1[layer].astype(jnp.float32)))
               - jnp.exp(jnp.sum(diff_lq2[layer].astype(jnp.float32) * diff_lk2[layer].astype(jnp.float32)))
               + lam_init)
        yd = diff_attention(qd.reshape(bsz, s, DIFF_HEADS, 2, DIFF_QK_DIM),
                            kd.reshape(bsz, s, DIFF_HEADS, 2, DIFF_QK_DIM),
                            vd.reshape(bsz, s, DIFF_HEADS, DIFF_V_DIM),
                            lam, lam_init, rel_bias[:, GROUP_HEADS:], diff_norm[layer])

        mix = jnp.concatenate([ya, yb, yc, yd], axis=-1)
        h = h + rms_norm(jnp.einsum('bsm,md->bsd', mix, w_out[layer]), norm_mix_post[layer])

        u = rms_norm(h, norm_ffn_pre[layer])
        f = jnp.square(jax.nn.relu(jnp.einsum('bsd,df->bsf', u, w_ff_up[layer])))
        h = h + rms_norm(jnp.einsum('bsf,fd->bsd', f, w_ff_down[layer]), norm_ffn_post[layer])
    return h
```

```python
import math
from contextlib import ExitStack
from types import SimpleNamespace

import numpy as np
import concourse.bass as bass
import concourse.mybir as mybir
from concourse.bass_utils import run_bass_kernel_spmd

F32 = mybir.dt.float32
BF16 = mybir.dt.bfloat16
AF = mybir.ActivationFunctionType
ALU = mybir.AluOpType
AX = mybir.AxisListType

SEM_CAP = 30000
EMBED_WAITS = True
_UID = [0]


def _u(n):
    _UID[0] += 1
    return f"{n}_{_UID[0]}"


class Ev:
    __slots__ = ("sem", "val", "eng", "dkey")

    def __init__(self):
        self.sem = None
        self.val = None
        self.eng = None
        self.dkey = None


class Ctx:
    def __init__(self, nc):
        self.nc = nc
        self.eng = {"pe": nc.tensor, "act": nc.scalar, "dve": nc.vector, "pool": nc.gpsimd, "sp": nc.sync}
        self.esems = {e: [] for e in ("pe", "act", "dve", "pool")}
        self.ecnt = {e: 0 for e in self.esems}
        self.waited = {e: {} for e in self.eng}
        self.lastw = {}
        self.reads = {}
        self.dsem = {}
        self.dcnt = {}
        self.pending_pe = []
        self.nsem = 0
        self.n_ops = 0
        self.n_waits = 0

    def _newsem(self, name):
        self.nsem += 1
        return self.nc.alloc_semaphore(name=f"{name}_{self.nsem}")

    def _deps(self, r, w):
        deps = []
        for t in r:
            e = self.lastw.get(t)
            if e is not None:
                deps.append(e)
        for t in w:
            e = self.lastw.get(t)
            if e is not None:
                deps.append(e)
            deps.extend(self.reads.get(t, {}).values())
        return deps

    def _wait(self, eng, deps, embed=False):
        wd = self.waited[eng]
        need = {}
        for ev in deps:
            if ev.eng == "pe" and eng == "pe":
                continue
            assert ev.sem is not None, "dependency on a PE op that has not signalled yet (mark sig=True)"
            val = ev.val
            if ev.dkey is not None:
                val = max(val, 16 * self.dcnt[ev.dkey])
            k = ev.sem.name
            if wd.get(k, 0) >= val:
                continue
            if k not in need or need[k][1] < val:
                need[k] = (ev.sem, val)
        items = list(need.items())
        emb = None
        if embed and items:
            emb = items.pop()
        for k, (sem, val) in items:
            self.eng[eng].wait_ge(sem, val)
            wd[k] = val
            self.n_waits += 1
        if emb is not None:
            wd[emb[0]] = emb[1][1]
            return emb[1]
        return None

    def _record(self, ev, r, w):
        rk = ev.eng if ev.dkey is None else ("dma", ev.dkey)
        for t in w:
            self.lastw[t] = ev
            self.reads[t] = {}
        for t in r:
            self.reads.setdefault(t, {})[rk] = ev

    def op(self, eng, fn, r=(), w=(), sig=True):
        emb = self._wait(eng, self._deps(r, w), embed=EMBED_WAITS)
        ins = fn()
        if emb is not None:
            ins._wait_ge(emb[0], emb[1])
        self.n_ops += 1
        ev = Ev()
        ev.eng = eng
        if sig:
            k = self.ecnt[eng]
            si, v = divmod(k, SEM_CAP)
            if si >= len(self.esems[eng]):
                self.esems[eng].append(self._newsem(f"e_{eng}"))
            sem = self.esems[eng][si]
            ins.then_inc(sem, 1)
            self.ecnt[eng] = k + 1
            ev.sem, ev.val = sem, v + 1
            if eng == "pe":
                for p in self.pending_pe:
                    p.sem, p.val = sem, v + 1
                self.pending_pe = []
        else:
            assert eng == "pe"
            self.pending_pe.append(ev)
        self._record(ev, r, w)
        return ins

    def dma(self, q, out, in_, r=(), w=(), key=None, **kw):
        assert key is not None
        self._wait(q, self._deps(r, w))
        if key not in self.dsem:
            self.dsem[key] = self._newsem("d")
            self.dcnt[key] = 0
        ins = self.eng[q].dma_start(out=out, in_=in_, **kw)
        self.dcnt[key] += 1
        ins.then_inc(self.dsem[key], 16)
        ev = Ev()
        ev.sem, ev.val, ev.eng, ev.dkey = self.dsem[key], 16 * self.dcnt[key], "dma", key
        self._record(ev, r, w)
        self.n_ops += 1
        return ins

    def coll(self, kind, in_ap, out_ap, groups, r=(), w=()):
        self._wait("pool", self._deps(r, w))
        if not hasattr(self, "csem"):
            self.csem = self._newsem("cc")
            self.ccnt = 0
        ins = self.nc.gpsimd.collective_compute(kind, mybir.AluOpType.bypass, replica_groups=groups,
                                                ins=[in_ap.opt()], outs=[out_ap.opt()])
        ins.then_inc(self.csem)
        self.ccnt += 1
        ev = Ev()
        ev.sem, ev.val, ev.eng = self.csem, self.ccnt, "cc"
        self._record(ev, r, w)
        self.n_ops += 1
        return ins

    def finish(self, q="sp"):
        deps = []
        for key, sem in self.dsem.items():
            ev = Ev()
            ev.sem, ev.val, ev.eng, ev.dkey = sem, 16 * self.dcnt[key], "dma", key
            deps.append(ev)
        for e, sems in self.esems.items():
            if sems:
                ev = Ev()
                k = self.ecnt[e]
                si, v = divmod(k - 1, SEM_CAP)
                ev.sem, ev.val, ev.eng = sems[si], v + 1, e
                deps.append(ev)
        self._wait(q, deps)


D_MODEL = 1024
D_FF = 4096
EPS = 1e-6
NEG = -30000.0
NPAR = 28
WD = 2432
LVD = WD + 127
WPD = WD + 128
WA = 256
LVA = WA + 127
WPA = WA + 128
FAR_D = 1664
PATS = ((128, 1), (512, 4), (2048, 16))


def t5_bucket_np(n):
    n = np.maximum(n, 0)
    nf = np.maximum(n, 1).astype(np.float32)
    large = 16 + (np.log(nf / np.float32(16)) / np.float32(math.log(2048 / 16)) * np.float32(16)).astype(np.int32)
    large = np.minimum(large, 31)
    return np.where(n < 16, n, large)


def build(S=8192, NHH=2, L=2, debug=False, phases=("p0", "p1", "A", "B", "C", "D", "p3"), pair=False):
    nc = bass.Bass("TRN2", target_bir_lowering=False)
    NB = S // 128
    NQ = S // 512
    NCOL = NHH * 1538
    NMIXL = NHH * 512
    NMIX = 1024
    MC = NMIX // 128
    SL = S // 2 if pair else S
    NQL = SL // 512
    GROUPS = [[0, 1], [2, 3], [4, 5], [6, 7]]
    assert (pair and NHH == 1) or (not pair and NHH == 2)
    c = Ctx(nc)

    def din(name, shape, dt=F32):
        return nc.dram_tensor(name, list(shape), dt, kind="ExternalInput")

    def dscr(name, shape, dt):
        return nc.dram_tensor(name, list(shape), dt, kind="ExternalOutput" if debug else "Internal")

    x_d = din("x", [SL, D_MODEL])
    rsel_d = din("rsel", [128, 2])
    win_d = din("w_in", [L, D_MODEL, NCOL])
    wout_d = din("w_out", [L, NMIX, D_MODEL])
    wup_d = din("w_up", [L, D_MODEL, D_FF])
    wdn_d = din("w_dn", [L, D_FF, D_MODEL])
    gains_d = din("gains", [L, 4, D_MODEL])
    pp_d = din("pp", [L, NHH, 128, NPAR])
    lruw_d = din("lruw", [L, NHH, 2, 128, 128])
    lam4_d = din("lam4", [L, 4, 32])
    bvd_d = din("bvd", [NHH * 2, LVD])
    bva_d = din("bva", [NHH * 2 * 3, LVA])
    cst_d = din("cst", [128, 8, 128])
    out_d = nc.dram_tensor("out", [SL, D_MODEL], F32, kind="ExternalOutput")

    TOKC = 1024 if pair else S
    NUK = SL // TOKC
    if pair:
        uTl_d = [nc.dram_tensor(f"uTl{k}", [D_MODEL, TOKC], BF16) for k in range(NUK)]
        uTg_d = [nc.dram_tensor(f"uTg{k}", [2 * D_MODEL, TOKC], BF16) for k in range(NUK)]
    else:
        uTl_d = [dscr("uT", [D_MODEL, S], BF16)]
        uTg_d = uTl_d
    ptb_d = dscr("ptb", [NHH, 4, 128, S], BF16)
    ptf_d = dscr("ptf", [NHH, 6, 128, S], F32)
    dtT_d = dscr("dtT", [NHH, 2, S], F32)
    va_d = dscr("va", [NHH, S, 128], BF16)
    vd_d = dscr("vd", [NHH, S, 128], BF16)
    if pair:
        mixT_d = [nc.dram_tensor(f"mixT{m}", [128, S], BF16) for m in range(4)]
        mixg_d = [nc.dram_tensor(f"mixg{m}", [256, S], BF16) for m in range(4)]
    else:
        mixT_d = [dscr("mixT", [NMIXL, S], BF16)]
        mixg_d = mixT_d

    def mix_dst(hh, m, r0, nrows, t0, t1):
        if pair:
            return mixT_d[m].ap()[r0:r0 + nrows, t0:t1]
        row0 = (hh * 4 + m) * 128 + r0
        return mixT_d[0].ap()[row0:row0 + nrows, t0:t1]

    def uT_store(q, uts_tile, rtag, key):
        k, off = divmod(q * 512, TOKC)
        c.dma("act", uTl_d[k].ap()[:, off:off + 512].rearrange("(c p) t -> p c t", p=128), uts_tile[:], r=[rtag], w=[("uTl", q)], key=key)
        if pair and (off + 512 == TOKC):
            qs_ = [k * (TOKC // 512) + i for i in range(TOKC // 512)]
            c.coll("AllGather", uTl_d[k].ap(), uTg_d[k].ap(), GROUPS, r=[("uTl", q_) for q_ in qs_],
                   w=[("uTg", rk_ * NQL + q_) for rk_ in range(2) for q_ in qs_])

    hbuf_d = dscr("hbuf", [SL, D_MODEL], F32)
    winb_d = nc.dram_tensor("winb", [L, D_MODEL, NCOL], BF16, kind="Internal")
    woutb_d = nc.dram_tensor("woutb", [L, NMIX, D_MODEL], BF16, kind="Internal")
    wupb_d = nc.dram_tensor("wupb", [L, D_MODEL, D_FF], BF16, kind="Internal")
    wdnb_d = nc.dram_tensor("wdnb", [L, D_FF, D_MODEL], BF16, kind="Internal")
    bandd_d = nc.dram_tensor("bandd", [NHH * 2, 130 * WPD], F32, kind="Internal")
    banda_d = nc.dram_tensor("banda", [NHH * 2 * 3, 130 * WPA], F32, kind="Internal")

    V = nc.vector
    A = nc.scalar
    G = nc.gpsimd
    T = nc.tensor

    def barrier():
        assert not c.pending_pe
        evs = []
        for key, sem in c.dsem.items():
            ev = Ev()
            ev.sem, ev.val, ev.eng, ev.dkey = sem, 16 * c.dcnt[key], "dma", key
            evs.append(ev)
        for e, sems in c.esems.items():
            if sems:
                ev = Ev()
                k = c.ecnt[e]
                si, v = divmod(k - 1, SEM_CAP)
                ev.sem, ev.val, ev.eng = sems[si], v + 1, "x" + e
                evs.append(ev)
        for e in ("pe", "act", "dve", "pool", "sp"):
            c._wait(e, evs)

    gs = ExitStack()

    def sbp(name, shape, dt):
        return gs.enter_context(nc.sbuf_tensor(_u(name), list(shape), dt))

    cst = sbp("cst", [128, 8, 128], F32)
    ident_b = sbp("ident_b", [128, 128], BF16)
    ones_b = sbp("ones_b", [128, 64], BF16)
    c.dma("sp", cst[:], cst_d.ap(), w=["cst"], key="cst")
    c.op("dve", lambda: V.tensor_copy(out=ident_b[:], in_=cst[:, 0, :]), r=["cst"], w=["ident_b"])
    c.op("dve", lambda: V.tensor_copy(out=ones_b[:], in_=cst[:, 2, 0:64]), r=["cst"], w=["ones_b"])
    ident_f = cst[:, 0, :]
    trimask = cst[:, 1, :]
    ones_f = cst[:, 2, :]

    epsc = sbp("epsc", [128, 2], F32)
    c.op("dve", lambda: V.memset(epsc[:, 0:1], EPS), w=["epsc"])
    c.op("dve", lambda: V.memset(epsc[:, 1:2], 1.0), w=["epsc"])

    def cast_in(l):
        for i in range(8):
            for a0 in range(0, NCOL, 2048):
                a1 = min(NCOL, a0 + 2048)
                c.dma("pool", winb_d.ap()[l, i * 128:(i + 1) * 128, a0:a1], win_d.ap()[l, i * 128:(i + 1) * 128, a0:a1], w=[("winb", l)], key=("winb", l))

    def cast_out(l):
        for i in range(NMIX // 128):
            c.dma("pool", woutb_d.ap()[l, i * 128:(i + 1) * 128, :], wout_d.ap()[l, i * 128:(i + 1) * 128, :], w=[("woutb", l)], key=("woutb", l))

    if "p1" in phases:
        cast_in(0)
    if "p3" in phases:
        cast_out(0)
    if "p3" in phases:
        for l in range(L):
            if l > 0:
                if "p1" in phases:
                    cast_in(l)
                cast_out(l)
            for i in range(8):
                c.dma("pool", wupb_d.ap()[l, i * 128:(i + 1) * 128, :].rearrange("p (a f) -> p a f", f=2048),
                      wup_d.ap()[l, i * 128:(i + 1) * 128, :].rearrange("p (a f) -> p a f", f=2048),
                      w=[("wupb", l)], key=("wupb", l))
            for i in range(32):
                c.dma("pool", wdnb_d.ap()[l, i * 128:(i + 1) * 128, :], wdn_d.ap()[l, i * 128:(i + 1) * 128, :],
                      w=[("wdnb", l)], key=("wdnb", l))

    for i in range(NHH * 2):
        dst = bass.AP(bandd_d, i * 130 * WPD + 1, [[WPD + 1, 128], [1, LVD]])
        src = bass.AP(bvd_d, i * LVD, [[0, 128], [1, LVD]])
        c.dma("sp", dst, src, w=[("bandd", i)], key="bandd")
    for i in range(NHH * 6):
        dst = bass.AP(banda_d, i * 130 * WPA + 1, [[WPA + 1, 128], [1, LVA]])
        src = bass.AP(bva_d, i * LVA, [[0, 128], [1, LVA]])
        c.dma("sp", dst, src, w=[("banda", i)], key="banda")

    def rstd_from_ss(ss_ap, out_ap, n, tag_in, tag_out, tmp_ap, tag_tmp):
        c.op("act", lambda: A.activation(out=tmp_ap, in_=ss_ap, func=AF.Sqrt, scale=1.0 / n, bias=epsc[:, 0:1]),
             r=[tag_in, "epsc"], w=[tag_tmp])
        c.op("dve", lambda: V.reciprocal(out=out_ap, in_=tmp_ap), r=[tag_tmp], w=[tag_out])


    def norm_to_uT(hsrc, htag, gt, gtag, ssq, ub, pst, uTs, uts_tag, j, k):
        jk = (j + k) % 2
        c.op("act", lambda: A.activation(out=ub[jk][:], in_=hsrc, func=AF.Square, accum_out=ssq[:, 0:1]),
             r=[htag], w=[("ub", jk), "ssq0"])
        rstd_from_ss(ssq[:, 0:1], ssq[:, 2:3], D_MODEL, "ssq0", "ssq2", ssq[:, 1:2], "ssq1")
        c.op("dve", lambda: V.scalar_tensor_tensor(out=ub[jk][:], in0=hsrc, scalar=ssq[:, 2:3], in1=gt, op0=ALU.mult, op1=ALU.mult),
             r=[htag, "ssq2", gtag], w=[("ub", jk)])
        for cc in range(8):
            c.op("pe", lambda cc=cc: T.transpose(pst[jk][:, cc * 128:(cc + 1) * 128], ub[jk][:, cc * 128:(cc + 1) * 128], ident_b[:]),
                 r=[("ub", jk), "ident_b"], w=[("pst", jk)], sig=(cc == 7))
        eng = "act" if jk == 0 else "dve"
        if eng == "act":
            c.op("act", lambda: A.copy(out=uTs[:, :, j * 128:(j + 1) * 128], in_=pst[jk][:].rearrange("p (c t) -> p c t", c=8)),
                 r=[("pst", jk)], w=[uts_tag])
        else:
            c.op("dve", lambda: V.tensor_copy(out=uTs[:, :, j * 128:(j + 1) * 128], in_=pst[jk][:].rearrange("p (c t) -> p c t", c=8)),
                 r=[("pst", jk)], w=[uts_tag])

    if "p0" in phases:
        with ExitStack() as es:
            sb = lambda n, s, d: es.enter_context(nc.sbuf_tensor(_u(n), list(s), d))
            ps = lambda n, s, d=F32: es.enter_context(nc.psum_tensor(_u(n), list(s), d))
            gt = sb("p0_g", [128, D_MODEL], F32)
            xt = [sb(f"p0_x{i}", [128, D_MODEL], F32) for i in range(2)]
            ub = [sb(f"p0_ub{i}", [128, D_MODEL], BF16) for i in range(2)]
            ssq = sb("p0_ssq", [128, 4], F32)
            uTs = [sb(f"p0_uTs{i}", [128, 8, 512], BF16) for i in range(2)]
            pst = [ps(f"p0_pst{i}", [128, 1024], BF16) for i in range(2)]
            c.dma("sp", gt[:], bass.AP(gains_d, 0, [[0, 128], [1, D_MODEL]]), w=["p0_g"], key="p0_g")
            for q in range(NQL):
                for j in range(4):
                    tb = q * 4 + j
                    c.dma("sp", xt[tb % 2][:], x_d.ap()[tb * 128:(tb + 1) * 128, :], w=[("p0_x", tb % 2)], key=("p0_x", tb % 2))
                    norm_to_uT(xt[tb % 2][:], ("p0_x", tb % 2), gt[:], "p0_g", ssq, ub, pst, uTs[q % 2], ("p0_uTs", q % 2), j, 0)
                uT_store(q, uTs[q % 2], ("p0_uTs", q % 2), ("p0_st", q % 2))
        barrier()

    for l in range(L):
        lam_init = 0.8 - 0.6 * math.exp(-0.3 * l)
        if "p1" in phases:
            with ExitStack() as es:
                sb = lambda n, s, d: es.enter_context(nc.sbuf_tensor(_u(n), list(s), d))
                ps = lambda n, s, d=F32: es.enter_context(nc.psum_tensor(_u(n), list(s), d))
                win = sb("p1_win", [128, 8, NCOL], BF16)
                c.dma("sp", win[:], winb_d.ap()[l].rearrange("(c p) n -> p c n", p=128), r=[("winb", l)], w=["p1_win"], key="p1_win")
                uTt = [sb(f"p1_uT{i}", [128, 8, 512], BF16) for i in range(2)]
                pj = [ps(f"p1_pj{i}", [128, 512]) for i in range(4)]
                stb = [sb(f"p1_stb{i}", [128, 512], BF16) for i in range(6)]
                stf = [sb(f"p1_stf{i}", [128, 512], F32) for i in range(6)]
                stv = [sb(f"p1_stv{i}", [128, NHH * 256], BF16) for i in range(2)]
                stdt = [sb(f"p1_stdt{i}", [2 * NHH, 512], F32) for i in range(2)]
                fmap = [("b", 0), ("b", 1), ("f", 0), ("f", 1), ("f", 2), ("f", 3), ("f", 4), ("f", 5), ("b", 2), ("b", 3)]
                cnt = 0
                nb_ = 0
                nf_ = 0
                nv_ = 0
                def load_u(q):
                    rk, ql = divmod(q, NQL)
                    k, off = divmod(ql * 512, TOKC)
                    c.dma("sp", uTt[q % 2][:], uTg_d[k].ap()[rk * D_MODEL:(rk + 1) * D_MODEL, off:off + 512].rearrange("(c p) t -> p c t", p=128),
                          r=[("uTg", q) if pair else ("uTl", q)], w=[("p1_uT", q % 2)], key=("p1_uT", q % 2))

                load_u(0)
                for q in range(NQ):
                    u = uTt[q % 2]
                    utag = ("p1_uT", q % 2)
                    if q + 1 < NQ:
                        load_u(q + 1)
                    for hh in range(NHH):
                        for ch in range(10):
                            col0 = (hh * 10 + ch) * 128
                            p = pj[cnt % 4]
                            ptag = ("p1_pj", cnt % 4)
                            cnt += 1
                            for cc in range(8):
                                c.op("pe", lambda cc=cc, p=p, col0=col0: T.matmul(p[:], lhsT=win[:, cc, col0:col0 + 128], rhs=u[:, cc, :], start=(cc == 0), stop=(cc == 7)),
                                     r=["p1_win", utag], w=[ptag], sig=(cc == 7))
                            kind, idx = fmap[ch]
                            if kind == "b":
                                st = stb[nb_ % 6]
                                stag = ("p1_stb", nb_ % 6)
                                nb_ += 1
                                c.op("act", lambda p=p, st=st: A.copy(out=st[:], in_=p[:]), r=[ptag], w=[stag])
                                c.dma("act", ptb_d.ap()[hh, idx, :, q * 512:(q + 1) * 512], st[:], r=[stag], w=[("ptb", hh, idx)], key=stag)
                            else:
                                st = stf[nf_ % 6]
                                stag = ("p1_stf", nf_ % 6)
                                nf_ += 1
                                c.op("dve", lambda p=p, st=st: V.tensor_copy(out=st[:], in_=p[:]), r=[ptag], w=[stag])
                                c.dma("act", ptf_d.ap()[hh, idx, :, q * 512:(q + 1) * 512], st[:], r=[stag], w=[("ptf", hh, idx)], key=stag)
                    vcol0 = NHH * 1280
                    for j in range(4):
                        p = pj[cnt % 4]
                        ptag = ("p1_pj", cnt % 4)
                        cnt += 1
                        for cc in range(8):
                            c.op("pe", lambda cc=cc, p=p, j=j: T.matmul(p[:, 0:NHH * 256], lhsT=u[:, cc, j * 128:(j + 1) * 128], rhs=win[:, cc, vcol0:vcol0 + NHH * 256], start=(cc == 0), stop=(cc == 7)),
                                 r=["p1_win", utag], w=[ptag], sig=(cc == 7))
                        st = stv[nv_ % 2]
                        stag = ("p1_stv", nv_ % 2)
                        nv_ += 1
                        c.op("act", lambda p=p, st=st: A.copy(out=st[:], in_=p[:, 0:NHH * 256]), r=[ptag], w=[stag])
                        tok0 = q * 512 + j * 128
                        for hh in range(NHH):
                            c.dma("act", va_d.ap()[hh, tok0:tok0 + 128, :], st[:, hh * 256:hh * 256 + 128], r=[stag], w=[("va", hh)], key=stag)
                            c.dma("act", vd_d.ap()[hh, tok0:tok0 + 128, :], st[:, hh * 256 + 128:hh * 256 + 256], r=[stag], w=[("vd", hh)], key=stag)
                    dcol0 = NHH * 1536
                    p = pj[cnt % 4]
                    ptag = ("p1_pj", cnt % 4)
                    cnt += 1
                    for cc in range(8):
                        c.op("pe", lambda cc=cc, p=p: T.matmul(p[0:2 * NHH, :], lhsT=win[:, cc, dcol0:dcol0 + 2 * NHH], rhs=u[:, cc, :], start=(cc == 0), stop=(cc == 7)),
                             r=["p1_win", utag], w=[ptag], sig=(cc == 7))
                    st = stdt[q % 2]
                    stag = ("p1_stdt", q % 2)
                    c.op("dve", lambda p=p, st=st: V.tensor_copy(out=st[:], in_=p[0:2 * NHH, :]), r=[ptag], w=[stag])
                    c.dma("act", dtT_d.ap()[:, :, q * 512:(q + 1) * 512].rearrange("h r t -> (h r) t"), st[:], r=[stag], w=["dtT"], key=stag)
            barrier()

        def gather_mix(m_):
            if pair:
                c.coll("AllGather", mixT_d[m_].ap(), mixg_d[m_].ap(), GROUPS, r=[("mixT", 0, m_)], w=[("mixg", m_)])

        for hh in range(NHH):
            if "B" in phases:
                mixer_B(nc, c, l, hh, S, locals())
                barrier()
                gather_mix(1)
            if "C" in phases:
                mixer_C(nc, c, l, hh, S, locals())
                barrier()
                gather_mix(2)
            if "A" in phases:
                mixer_A(nc, c, l, hh, S, locals())
                barrier()
                gather_mix(0)
            if "D" in phases:
                mixer_D(nc, c, l, hh, S, lam_init, locals())
                barrier()
                gather_mix(3)

        if "p3" in phases:
            phase_p3(nc, c, l, S, L, locals())
            barrier()

    c.finish("sp")
    gs.close()
    return nc, c


def mixer_B(nc, c, l, hh, S, env):
    e = SimpleNamespace(**env)
    V, A, G, T = nc.vector, nc.scalar, nc.gpsimd, nc.tensor
    TS = min(2048, S)
    NSEG = S // TS
    with ExitStack() as es:
        sb = lambda n, s, d: es.enter_context(nc.sbuf_tensor(_u(n), list(s), d))
        ps = lambda n, s, d=F32: es.enter_context(nc.psum_tensor(_u(n), list(s), d))
        pp = sb("b_pp", [128, NPAR], F32)
        wf = sb("b_wf", [128, 2, 128], F32)
        wb = sb("b_wb", [128, 2, 128], BF16)
        nsp = sb("b_nsp", [128, 4], F32)
        xr = sb("b_xr", [128, TS + 4], F32)
        xg = sb("b_xg", [128, TS], F32)
        xc = sb("b_xc", [128, TS], F32)
        xcb = sb("b_xcb", [128, TS], BF16)
        rr = sb("b_r", [128, TS], F32)
        ii = sb("b_i", [128, TS], F32)
        aa = sb("b_a", [128, TS], F32)
        hs = sb("b_hs", [128, TS], F32)
        yb = sb("b_yb", [128, TS], BF16)
        carry = sb("b_carry", [128, 2], F32)
        pg = [ps(f"b_pg{i}", [128, 512]) for i in range(4)]
        c.dma("sp", pp[:], e.pp_d.ap()[l, hh], w=["b_pp"], key="b_pp")
        c.dma("sp", wf[:], e.lruw_d.ap()[l, hh].rearrange("a p j -> p a j"), w=["b_wf"], key="b_wf")
        c.op("dve", lambda: V.tensor_copy(out=wb[:], in_=wf[:]), r=["b_wf"], w=["b_wb"])
        c.op("act", lambda: A.activation(out=nsp[:, 0:1], in_=pp[:, 7:8], func=AF.Exp, scale=-1.0), r=["b_pp"], w=["b_nsp0"])
        c.op("act", lambda: A.activation(out=nsp[:, 1:2], in_=nsp[:, 0:1], func=AF.Ln, bias=e.epsc[:, 1:2], scale=1.0), r=["b_nsp0", "epsc"], w=["b_nsp1"])
        c.op("dve", lambda: V.tensor_scalar(out=nsp[:, 2:3], in0=nsp[:, 1:2], scalar1=-8.0, scalar2=None, op0=ALU.mult), r=["b_nsp1"], w=["b_nsp2"])
        c.op("dve", lambda: V.tensor_scalar(out=nsp[:, 3:4], in0=nsp[:, 1:2], scalar1=-16.0, scalar2=None, op0=ALU.mult), r=["b_nsp1"], w=["b_nsp3"])
        c.op("pool", lambda: G.memset(carry[:], 0.0), w=["b_carry"])
        gate_d = e.ptf_d.ap()[hh, 0]
        xr_d = e.ptf_d.ap()[hh, 1]
        for sg in range(NSEG):
            t0 = sg * TS
            if sg == 0:
                c.op("pool", lambda: G.memset(xr[:, 0:4], 0.0), w=["b_xr"])
                c.dma("sp", xr[:, 4:4 + TS], xr_d[:, 0:TS], r=[("ptf", hh, 1)], w=["b_xr"], key="b_xr")
            else:
                c.dma("sp", xr[:, 1:4 + TS], xr_d[:, t0 - 3:t0 + TS], r=[("ptf", hh, 1)], w=["b_xr"], key="b_xr")
            c.dma("sp", xg[:], gate_d[:, t0:t0 + TS], r=[("ptf", hh, 0)], w=["b_xg"], key="b_xg")
            c.op("dve", lambda: V.tensor_scalar(out=xc[:], in0=xr[:, 1:1 + TS], scalar1=pp[:, 0:1], scalar2=pp[:, 4:5], op0=ALU.mult, op1=ALU.add),
                 r=["b_xr", "b_pp"], w=["b_xc"])
            for k in range(1, 4):
                c.op("dve", lambda k=k: V.scalar_tensor_tensor(out=xc[:], in0=xr[:, 1 + k:1 + k + TS], scalar=pp[:, k:k + 1], in1=xc[:], op0=ALU.mult, op1=ALU.add),
                     r=["b_xr", "b_pp", "b_xc"], w=["b_xc"])
            c.op("act", lambda: A.copy(out=xcb[:], in_=xc[:]), r=["b_xc"], w=["b_xcb"])
            for j in range(TS // 512):
                sl = slice(j * 512, (j + 1) * 512)
                pr, pi = pg[(2 * j) % 4], pg[(2 * j + 1) % 4]
                tr, ti = ("b_pg", (2 * j) % 4), ("b_pg", (2 * j + 1) % 4)
                c.op("pe", lambda pr=pr, sl=sl: T.matmul(pr[:], lhsT=wb[:, 0, :], rhs=xcb[:, sl], start=True, stop=True), r=["b_wb", "b_xcb"], w=[tr])
                c.op("pe", lambda pi=pi, sl=sl: T.matmul(pi[:], lhsT=wb[:, 1, :], rhs=xcb[:, sl], start=True, stop=True), r=["b_wb", "b_xcb"], w=[ti])
                c.op("act", lambda pr=pr, sl=sl: A.activation(out=rr[:, sl], in_=pr[:], func=AF.Sigmoid, bias=pp[:, 5:6], scale=1.0), r=[tr, "b_pp"], w=["b_r"])
                c.op("act", lambda pi=pi, sl=sl: A.activation(out=ii[:, sl], in_=pi[:], func=AF.Sigmoid, bias=pp[:, 6:7], scale=1.0), r=[ti, "b_pp"], w=["b_i"])
            c.op("act", lambda: A.activation(out=aa[:], in_=rr[:], func=AF.Exp, scale=nsp[:, 2:3]), r=["b_r", "b_nsp2"], w=["b_a"])
            c.op("act", lambda: A.activation(out=rr[:], in_=rr[:], func=AF.Exp, scale=nsp[:, 3:4]), r=["b_r", "b_nsp3"], w=["b_r"])
            c.op("act", lambda: A.activation(out=rr[:], in_=rr[:], func=AF.Sqrt, scale=-1.0, bias=e.epsc[:, 1:2]), r=["b_r", "epsc"], w=["b_r"])
            c.op("pool", lambda: G.tensor_tensor(out=ii[:], in0=ii[:], in1=xc[:], op=ALU.mult), r=["b_i", "b_xc"], w=["b_i"])
            c.op("dve", lambda: V.tensor_tensor(out=ii[:], in0=ii[:], in1=rr[:], op=ALU.mult), r=["b_i", "b_r"], w=["b_i"])
            ci = sg % 2
            c.op("dve", lambda ci=ci: V.tensor_tensor_scan(out=hs[:], data0=aa[:], data1=ii[:], initial=carry[:, ci:ci + 1], op0=ALU.mult, op1=ALU.add),
                 r=["b_a", "b_i", "b_carry"], w=["b_hs"])
            c.op("dve", lambda ci=ci: V.tensor_copy(out=carry[:, 1 - ci:2 - ci], in_=hs[:, TS - 1:TS]), r=["b_hs"], w=["b_carry"])
            c.op("act", lambda: A.activation(out=xg[:], in_=xg[:], func=AF.Gelu_apprx_tanh), r=["b_xg"], w=["b_xg"])
            c.op("dve", lambda: V.tensor_tensor(out=yb[:], in0=xg[:], in1=hs[:], op=ALU.mult), r=["b_xg", "b_hs"], w=["b_yb"])
            c.dma("act", e.mix_dst(hh, 1, 0, 128, t0, t0 + TS), yb[:], r=["b_yb"], w=[("mixT", hh, 1)], key="b_st")


def mixer_C(nc, c, l, hh, S, env):
    e = SimpleNamespace(**env)
    V, A, G, T = nc.vector, nc.scalar, nc.gpsimd, nc.tensor
    TS = min(2048, S)
    NSEG = S // TS
    NCH = S // 128
    cst = e.cst
    with ExitStack() as es:
        sb = lambda n, s, d: es.enter_context(nc.sbuf_tensor(_u(n), list(s), d))
        ps = lambda n, s, d=F32: es.enter_context(nc.psum_tensor(_u(n), list(s), d))
        pp = sb("c_pp", [128, NPAR], F32)
        raw = sb("c_raw", [128, TS + 4], F32)
        cv = sb("c_cv", [128, TS], F32)
        xsf = sb("c_xsf", [128, TS], F32)
        xsb = sb("c_xsb", [128, TS], BF16)
        Bb = sb("c_Bb", [128, TS], BF16)
        Cb = sb("c_Cb", [128, TS], BF16)
        sz = sb("c_sz", [128, TS], F32)
        dtcs = sb("c_dtcs", [34, TS], F32)
        adt = sb("c_adt", [34, TS], F32)
        onesT = sb("c_ones", [34, TS], F32)
        acol = sb("c_acol", [34, 2], F32)
        cscarry = sb("c_cscarry", [34, 2], F32)
        ncsend = sb("c_ncsend", [128, 2, NCH + 1], F32)
        xBt = [sb(f"c_xBt{i}", [128, 256], BF16) for i in range(2)]
        colsb = [sb(f"c_cols{i}", [128, 4], F32) for i in range(2)]
        ncs = [sb(f"c_ncs{i}", [128, 2], F32) for i in range(2)]
        tmpL = [sb(f"c_tmpL{i}", [128, 128], F32) for i in range(2)]
        LT = [sb(f"c_LT{i}", [128, 128], F32) for i in range(2)]
        MT = [sb(f"c_MT{i}", [128, 128], BF16) for i in range(2)]
        Er = [sb(f"c_Er{i}", [128, 128], F32) for i in range(2)]
        CsT = [sb(f"c_CsT{i}", [128, 128], BF16) for i in range(2)]
        xdt = [sb(f"c_xdt{i}", [128, 64], BF16) for i in range(2)]
        xdd = [sb(f"c_xdd{i}", [128, 64], BF16) for i in range(2)]
        state = [sb(f"c_state{i}", [128, 64], F32) for i in range(2)]
        prevT = [sb(f"c_prevT{i}", [128, 64], BF16) for i in range(2)]
        ysb = sb("c_ysb", [128, 512], F32)
        sq = sb("c_sq", [128, 512], F32)
        rs = sb("c_rs", [128, 512], F32)
        yob = [sb(f"c_yob{i}", [128, 512], BF16) for i in range(2)]
        rt = [ps(f"c_rt{i}", [128, 512]) for i in range(2)]
        yps = ps("c_yps", [128, 512])
        ssq = ps("c_ssq", [128, 512])
        gt = [ps(f"c_gt{i}", [128, 512]) for i in range(2)]
        trp = ps("c_trp", [128, 1024], BF16)
        misc = ps("c_misc", [128, 512])

        c.dma("sp", pp[:], e.pp_d.ap()[l, hh], w=["c_pp"], key="c_pp")
        c.op("pool", lambda: G.memset(dtcs[:], 0.0), w=["c_dtcs"])
        c.op("pool", lambda: G.memset(adt[:], 0.0), w=["c_adt"])
        c.op("pool", lambda: G.memset(onesT[:], 1.0), w=["c_ones"])
        c.op("pool", lambda: G.memset(cscarry[:], 0.0), w=["c_cscarry"])
        c.op("pool", lambda: G.memset(ncsend[:], 0.0), w=["c_ncsend"])
        for r in range(2):
            c.op("pool", lambda r=r: G.memset(state[r][:], 0.0), w=[("c_state", r)])
            c.op("pool", lambda r=r: G.memset(prevT[r][:], 0.0), w=[("c_prevT", r)])
        c.op("act", lambda: A.activation(out=acol[:, 0:1], in_=pp[0:34, 27:28], func=AF.Exp), r=["c_pp"], w=["c_acol0"])
        c.op("dve", lambda: V.tensor_scalar(out=acol[:, 1:2], in0=acol[:, 0:1], scalar1=-1.0, scalar2=None, op0=ALU.mult), r=["c_acol0"], w=["c_acol"])

        def conv_silu(src_d, wcol, dst_f, dst_b, t0, rtag):
            if t0 == 0:
                c.op("pool", lambda: G.memset(raw[:, 0:4], 0.0), w=["c_raw"])
                c.dma("sp", raw[:, 4:4 + TS], src_d[:, 0:TS], r=[rtag], w=["c_raw"], key="c_raw")
            else:
                c.dma("sp", raw[:, 1:4 + TS], src_d[:, t0 - 3:t0 + TS], r=[rtag], w=["c_raw"], key="c_raw")
            c.op("dve", lambda: V.tensor_scalar(out=cv[:], in0=raw[:, 1:1 + TS], scalar1=pp[:, wcol:wcol + 1], scalar2=pp[:, wcol + 4:wcol + 5], op0=ALU.mult, op1=ALU.add),
                 r=["c_raw", "c_pp"], w=["c_cv"])
            for k in range(1, 4):
                eng = "dve"
                EE = V
                c.op(eng, lambda k=k, EE=EE: EE.scalar_tensor_tensor(out=cv[:], in0=raw[:, 1 + k:1 + k + TS], scalar=pp[:, wcol + k:wcol + k + 1], in1=cv[:], op0=ALU.mult, op1=ALU.add),
                     r=["c_raw", "c_pp", "c_cv"], w=["c_cv"])
            if dst_f is not None:
                c.op("act", lambda: A.activation(out=dst_f[0][:], in_=cv[:], func=AF.Silu), r=["c_cv"], w=[dst_f[1]])
                c.op("dve", lambda: V.tensor_copy(out=dst_b[0][:], in_=dst_f[0][:]), r=[dst_f[1]], w=[dst_b[1]])
            else:
                c.op("act", lambda: A.activation(out=dst_b[0][:], in_=cv[:], func=AF.Silu), r=["c_cv"], w=[dst_b[1]])

        gchunk = 0
        for sg in range(NSEG):
            t0 = sg * TS
            conv_silu(e.ptf_d.ap()[hh, 3], 8, (xsf, "c_xsf"), (xsb, "c_xsb"), t0, ("ptf", hh, 3))
            conv_silu(e.ptf_d.ap()[hh, 4], 13, None, (Bb, "c_Bb"), t0, ("ptf", hh, 4))
            conv_silu(e.ptf_d.ap()[hh, 5], 18, None, (Cb, "c_Cb"), t0, ("ptf", hh, 5))
            c.dma("sp", sz[:], e.ptf_d.ap()[hh, 2][:, t0:t0 + TS], r=[("ptf", hh, 2)], w=["c_sz"], key="c_sz")
            c.op("act", lambda: A.activation(out=sz[:], in_=sz[:], func=AF.Silu), r=["c_sz"], w=["c_sz"])
            c.dma("sp", dtcs[0:2, :], e.dtT_d.ap()[hh, :, t0:t0 + TS], r=["dtT"], w=["c_dtcs"], key="c_dt")
            c.dma("sp", dtcs[32:34, :], e.dtT_d.ap()[hh, :, t0:t0 + TS], r=["dtT"], w=["c_dtcs"], key="c_dt")
            c.op("act", lambda: A.activation(out=dtcs[:], in_=dtcs[:], func=AF.Exp, bias=pp[0:34, 26:27], scale=1.0), r=["c_dtcs", "c_pp"], w=["c_dtcs"])
            c.op("act", lambda: A.activation(out=dtcs[:], in_=dtcs[:], func=AF.Ln, bias=e.epsc[0:34, 1:2], scale=1.0), r=["c_dtcs", "epsc"], w=["c_dtcs"])
            c.op("dve", lambda: V.tensor_scalar(out=adt[32:34, :], in0=dtcs[32:34, :], scalar1=acol[32:34, 1:2], scalar2=None, op0=ALU.mult),
                 r=["c_dtcs", "c_acol"], w=["c_adt"])
            ci = sg % 2
            c.op("dve", lambda ci=ci: V.tensor_tensor_scan(out=dtcs[32:34, :], data0=onesT[32:34, :], data1=adt[32:34, :], initial=cscarry[32:34, ci:ci + 1], op0=ALU.mult, op1=ALU.add),
                 r=["c_ones", "c_adt", "c_cscarry"], w=["c_dtcs"])
            c.op("dve", lambda ci=ci: V.tensor_copy(out=cscarry[32:34, 1 - ci:2 - ci], in_=dtcs[32:34, TS - 1:TS]), r=["c_dtcs"], w=["c_cscarry"])
            for g in range(TS // 512):
                gsl = slice(g * 512, (g + 1) * 512)
                for r in range(2):
                    c.op("pe", lambda r=r, gsl=gsl: T.matmul(rt[r][:], lhsT=cst[0:34, 4 + r, :], rhs=dtcs[:, gsl], start=True, stop=True),
                         r=["cst", "c_dtcs"], w=[("c_rt", r)])
                    c.op("dve", lambda r=r: V.tensor_scalar(out=ncsend[:, r, gchunk + 1:gchunk + 5], in0=rt[r][:, 127:512:128], scalar1=-1.0, scalar2=None, op0=ALU.mult),
                         r=[("c_rt", r)], w=["c_ncsend"])
                for ck in range(4):
                    k2 = gchunk % 2
                    sl = slice(g * 512 + ck * 128, g * 512 + (ck + 1) * 128)
                    csl = slice(ck * 128, (ck + 1) * 128)
                    c.op("pe", lambda sl=sl: T.transpose(trp[:, 0:128], xsb[:, sl], e.ident_b[:]), r=["c_xsb", "ident_b"], w=["c_trp"], sig=False)
                    c.op("pe", lambda sl=sl: T.transpose(trp[:, 128:256], Bb[:, sl], e.ident_b[:]), r=["c_Bb", "ident_b"], w=["c_trp"])
                    c.op("act", lambda k2=k2: A.copy(out=xBt[k2][:], in_=trp[:, 0:256]), r=["c_trp"], w=[("c_xBt", k2)])
                    c.op("pe", lambda sl=sl: T.matmul(misc[:, 128:132], lhsT=dtcs[:, sl], rhs=cst[0:34, 3, 0:4], start=True, stop=True),
                         r=["c_dtcs", "cst"], w=["c_misc_c"])
                    c.op("dve", lambda k2=k2: V.tensor_copy(out=colsb[k2][:], in_=misc[:, 128:132]), r=["c_misc_c"], w=[("c_cols", k2)])
                    c.op("dve", lambda k2=k2: V.tensor_scalar(out=ncs[k2][:], in0=colsb[k2][:, 2:4], scalar1=-1.0, scalar2=None, op0=ALU.mult),
                         r=[("c_cols", k2)], w=[("c_ncs", k2)])
                    c.op("pe", lambda k2=k2, sl=sl: T.matmul(gt[k2][:, 0:128], lhsT=Bb[:, sl], rhs=Cb[:, sl], start=True, stop=True),
                         r=["c_Bb", "c_Cb"], w=[("c_gt", k2)])
                    for r in range(2):
                        c.op("dve", lambda r=r, csl=csl: V.tensor_tensor(out=tmpL[r][:], in0=rt[r][:, csl], in1=cst[:, 1, :], op=ALU.add),
                             r=[("c_rt", r), "cst"], w=[("c_tmpL", r)])
                        c.op("act", lambda r=r, k2=k2: A.activation(out=LT[r][:], in_=tmpL[r][:], func=AF.Exp, bias=ncs[k2][:, r:r + 1], scale=1.0),
                             r=[("c_tmpL", r), ("c_ncs", k2)], w=[("c_LT", r)])
                        c.op("dve", lambda r=r, k2=k2: V.tensor_tensor(out=MT[r][:], in0=gt[k2][:, 0:128], in1=LT[r][:], op=ALU.mult),
                             r=[("c_gt", k2), ("c_LT", r)], w=[("c_MT", r)])
                        c.op("dve", lambda r=r, k2=k2: V.tensor_scalar(out=xdt[r][:], in0=xBt[k2][:, 64 * r:64 * r + 64], scalar1=colsb[k2][:, r:r + 1], scalar2=None, op0=ALU.mult),
                             r=[("c_xBt", k2), ("c_cols", k2)], w=[("c_xdt", r)])
                        c.op("dve", lambda r=r, k2=k2: V.tensor_scalar(out=xdd[r][:], in0=xBt[k2][:, 64 * r:64 * r + 64], scalar1=colsb[k2][:, r:r + 1], scalar2=LT[r][:, 127:128], op0=ALU.mult, op1=ALU.mult),
                             r=[("c_xBt", k2), ("c_cols", k2), ("c_LT", r)], w=[("c_xdd", r)])
                        c.op("act", lambda r=r, csl=csl: A.activation(out=Er[r][:], in_=rt[r][:, csl], func=AF.Exp, bias=ncsend[:, r, gchunk:gchunk + 1], scale=1.0),
                             r=[("c_rt", r), "c_ncsend"], w=[("c_Er", r)])
                        c.op("pool", lambda r=r, sl=sl: G.tensor_tensor(out=CsT[r][:], in0=Cb[:, sl], in1=Er[r][:], op=ALU.mult),
                             r=["c_Cb", ("c_Er", r)], w=[("c_CsT", r)])
                        c.op("pe", lambda r=r, csl=csl: T.matmul(yps[64 * r:64 * r + 64, csl], lhsT=xdt[r][:], rhs=MT[r][:], start=True, stop=False),
                             r=[("c_xdt", r), ("c_MT", r)], w=["c_yps"], sig=False)
                        c.op("pe", lambda r=r, csl=csl: T.matmul(yps[64 * r:64 * r + 64, csl], lhsT=prevT[r][:], rhs=CsT[r][:], start=False, stop=True),
                             r=[("c_prevT", r), ("c_CsT", r)], w=["c_yps"])
                        c.op("pe", lambda r=r, k2=k2: T.matmul(misc[:, 64 * r:64 * r + 64], lhsT=xBt[k2][:, 128:256], rhs=xdd[r][:], start=True, stop=True),
                             r=[("c_xBt", k2), ("c_xdd", r)], w=[("c_misc_s", r)])
                        c.op("dve", lambda r=r: V.scalar_tensor_tensor(out=state[r][:], in0=state[r][:], scalar=Er[r][:, 127:128], in1=misc[:, 64 * r:64 * r + 64], op0=ALU.mult, op1=ALU.add),
                             r=[("c_state", r), ("c_Er", r), ("c_misc_s", r)], w=[("c_state", r)])
                        c.op("act", lambda r=r: A.copy(out=prevT[r][:], in_=state[r][:]), r=[("c_state", r)], w=[("c_prevT", r)])
                    gchunk += 1
                gi = (sg * (TS // 512) + g) % 2
                c.op("dve", lambda gsl=gsl: V.scalar_tensor_tensor(out=ysb[:], in0=xsf[:, gsl], scalar=pp[:, 24:25], in1=yps[:], op0=ALU.mult, op1=ALU.add),
                     r=["c_xsf", "c_pp", "c_yps"], w=["c_ysb"])
                c.op("pool", lambda gsl=gsl: G.tensor_tensor(out=ysb[:], in0=ysb[:], in1=sz[:, gsl], op=ALU.mult), r=["c_ysb", "c_sz"], w=["c_ysb"])
                c.op("act", lambda: A.activation(out=sq[:], in_=ysb[:], func=AF.Square), r=["c_ysb"], w=["c_sq"])
                c.op("pe", lambda: T.matmul(ssq[:], lhsT=cst[:, 2, :], rhs=sq[:], start=True, stop=True), r=["cst", "c_sq"], w=["c_ssq"])
                c.op("act", lambda: A.activation(out=rs[:], in_=ssq[:], func=AF.Ln, scale=1.0 / 128, bias=e.epsc[:, 0:1]), r=["c_ssq", "epsc"], w=["c_rs"])
                c.op("act", lambda: A.activation(out=rs[:], in_=rs[:], func=AF.Exp, scale=-0.5), r=["c_rs"], w=["c_rs"])
                c.op("dve", lambda gi=gi: V.scalar_tensor_tensor(out=yob[gi][:], in0=ysb[:], scalar=pp[:, 23:24], in1=rs[:], op0=ALU.mult, op1=ALU.mult),
                     r=["c_ysb", "c_pp", "c_rs"], w=[("c_yob", gi)])
                tok0 = t0 + g * 512
                c.dma("act", e.mix_dst(hh, 2, 0, 128, tok0, tok0 + 512), yob[gi][:], r=[("c_yob", gi)], w=[("mixT", hh, 2)], key=("c_st", gi))


def mixer_D(nc, c, l, hh, S, lam_init, env):
    e = SimpleNamespace(**env)
    V, A, G, T = nc.vector, nc.scalar, nc.gpsimd, nc.tensor
    NB = S // 128
    NQ = S // 512
    scale = 32.0 ** -0.5
    cst = e.cst
    with ExitStack() as es:
        sb = lambda n, s, d: es.enter_context(nc.sbuf_tensor(_u(n), list(s), d))
        ps = lambda n, s, d=F32: es.enter_context(nc.psum_tensor(_u(n), list(s), d))
        pp = sb("d_pp", [128, NPAR], F32)
        kT = sb("d_kT", [128, S], BF16)
        qT = sb("d_qT", [128, S], BF16)
        Va = sb("d_Va", [128, NB, 2, 128], BF16)
        Rsb = [sb(f"d_R{h}", [128, WD], F32) for h in range(2)]
        c31 = sb("d_c31", [128, 2], F32)
        qm = [sb(f"d_qm{i}", [128, 4, 512], BF16) for i in range(2)]
        Et = [sb(f"d_E{i}", [128, 2, 512], BF16) for i in range(3)]
        tmp = [sb(f"d_tmp{i}", [128, 2, 512], F32) for i in range(2)]
        lamt = sb("d_lamt", [1, 160], F32)
        nlamc = sb("d_nlamc", [128, 2], F32)
        rc = [sb(f"d_rc{i}", [64, 512], F32) for i in range(2)]
        oo = [sb(f"d_o{i}", [64, 512], F32) for i in range(2)]
        od = sb("d_od", [64, 512], F32)
        sqd = sb("d_sqd", [64, 512], F32)
        rsd = sb("d_rsd", [64, 512], F32)
        yb = [sb(f"d_yb{i}", [64, 512], BF16) for i in range(2)]
        pall = ps("d_pall", [128, 8, 512])
        st = [pall[:, 0:2, :], pall[:, 2:4, :], pall[:, 4:6, :]]
        acc = [pall[:, 6, :], pall[:, 7, :]]
        ssq = pall[:, 0, :]
        pl = ssq

        c.dma("sp", pp[:], e.pp_d.ap()[l, hh], w=["d_pp"], key="d_pp")
        c.dma("sp", kT[:], e.ptb_d.ap()[hh, 3], r=[("ptb", hh, 3)], w=["d_kT"], key="d_kT")
        c.dma("sp", qT[:], e.ptb_d.ap()[hh, 2], r=[("ptb", hh, 2)], w=["d_qT"], key="d_qT")
        c.op("pool", lambda: G.memset(Va[:, :, :, 64:128], 1.0), w=["d_Va1"])
        for h in range(2):
            c.dma("sp", Va[:, :, h, 0:64], e.vd_d.ap()[hh, :, h * 64:(h + 1) * 64].rearrange("(n p) d -> p n d", p=128),
                  r=[("vd", hh)], w=["d_Va0"], key="d_Va")
            bi = hh * 2 + h
            c.dma("sp", Rsb[h][:], bass.AP(e.bandd_d, bi * 130 * WPD + 128, [[WPD, 128], [1, WD]]), r=[("bandd", bi)], w=[("d_R", h)], key="d_R")
            c.dma("sp", c31[:, h:h + 1], bass.AP(e.bvd_d, bi * LVD + LVD - 1, [[0, 128], [1, 1]]), w=["d_c31"], key="d_c31")
        for i in range(2):
            c.op("pool", lambda i=i: G.memset(qm[i][:], 0.0), w=[("d_qm", i)])
        c.dma("sp", lamt[0:1, 0:128], e.lam4_d.ap()[l:l + 1].rearrange("a f k -> a (f k)"), w=["d_lamt"], key="d_lamt")
        lv = lamt[0:1, 0:128].rearrange("p (a b k) -> p a b k", a=2, b=2)
        prod = sb("d_prod", [1, 2, 32], F32)
        red = sb("d_red", [1, 8], F32)
        c.op("dve", lambda: V.tensor_tensor(out=prod[:], in0=lv[:, :, 0, :], in1=lv[:, :, 1, :], op=ALU.mult), r=["d_lamt"], w=["d_prod"])
        c.op("dve", lambda: V.tensor_reduce(out=red[:, 0:2], in_=prod[:], op=ALU.add, axis=AX.X), r=["d_prod"], w=["d_red0"])
        c.op("act", lambda: A.activation(out=red[:, 2:4], in_=red[:, 0:2], func=AF.Exp), r=["d_red0"], w=["d_red1"])
        c.op("dve", lambda: V.tensor_tensor(out=red[:, 4:5], in0=red[:, 3:4], in1=red[:, 2:3], op=ALU.subtract), r=["d_red1"], w=["d_red2"])
        c.op("dve", lambda: V.tensor_scalar(out=red[:, 5:6], in0=red[:, 4:5], scalar1=-lam_init, scalar2=None, op0=ALU.add), r=["d_red2"], w=["d_red3"])
        c.op("dve", lambda: V.tensor_copy(out=red[:, 6:7], in_=red[:, 5:6]), r=["d_red3"], w=["d_red4"])
        c.op("pe", lambda: T.matmul(pl[:, 0:2], lhsT=cst[0:1, 2, :], rhs=red[0:1, 5:7], start=True, stop=True), r=["cst", "d_red4"], w=[("d_st", 0)])
        c.op("dve", lambda: V.tensor_copy(out=nlamc[:], in_=pl[:, 0:2]), r=[("d_st", 0)], w=["d_nlamc"])
        gcol = sb("d_gcol", [128, 1], F32)
        c.op("dve", lambda: V.tensor_scalar(out=gcol[:], in0=pp[:, 25:26], scalar1=1.0 - lam_init, scalar2=None, op0=ALU.mult), r=["d_pp"], w=["d_gcol"])

        nst = 0
        nE = 0
        ntmp = 0
        fin = 0
        LA = 2
        ssq = acc[0]
        SSQT = ("d_acc", 0)

        def qm_fill(Q):
            qmq = qm[Q % 2]
            for hc in range(4):
                if hc % 2 == 0:
                    c.op("act", lambda hc=hc: A.copy(out=qmq[32 * hc:32 * hc + 32, hc, :], in_=qT[32 * hc:32 * hc + 32, Q * 512:(Q + 1) * 512]), r=["d_qT"], w=[("d_qm", Q % 2)])
                else:
                    c.op("pool", lambda hc=hc: G.tensor_copy(out=qmq[32 * hc:32 * hc + 32, hc, :], in_=qT[32 * hc:32 * hc + 32, Q * 512:(Q + 1) * 512]), r=["d_qT"], w=[("d_qm", Q % 2)])

        def qk(Q, h, kb, si):
            for cp in range(2):
                c.op("pe", lambda cp=cp: T.matmul(st[si][:, cp, :], lhsT=kT[:, kb * 128:(kb + 1) * 128], rhs=qm[Q % 2][:, 2 * h + cp, :], start=True, stop=True),
                     r=["d_kT", ("d_qm", Q % 2)], w=[("d_st", si)], sig=(cp == 1))

        def prologue(Q, h):
            for k in range(min(LA, 4 * Q + 4)):
                qk(Q, h, k, (nst + k) % 3)

        QH = [(Q, h) for Q in range(NQ) for h in range(2)]
        qm_fill(0)
        prologue(0, 0)
        for qi, (Q, h) in enumerate(QH):
            nkb = 4 * Q + 4
            if h == 1 and Q + 1 < NQ:
                qm_fill(Q + 1)
            for kb in range(nkb):
                si = nst % 3
                nst += 1
                if kb + LA < nkb:
                    qk(Q, h, kb + LA, (si + LA) % 3)
                ei = nE % 3
                nE += 1
                D = Q * 512 - kb * 128
                if D >= FAR_D:
                    c.op("act", lambda: A.activation(out=Et[ei][:], in_=st[si][:], func=AF.Exp, scale=scale, bias=c31[:, h:h + 1]),
                         r=[("d_st", si), "d_c31"], w=[("d_E", ei)])
                else:
                    ti = ntmp % 2
                    ntmp += 1
                    rb = bass.AP(Rsb[h], D + 384, [[WD, 128], [0, 2], [1, 512]])
                    c.op("dve", lambda: V.scalar_tensor_tensor(out=tmp[ti][:], in0=st[si][:], scalar=scale, in1=rb, op0=ALU.mult, op1=ALU.add),
                         r=[("d_st", si), ("d_R", h)], w=[("d_tmp", ti)])
                    c.op("act", lambda: A.activation(out=Et[ei][:], in_=tmp[ti][:], func=AF.Exp), r=[("d_tmp", ti)], w=[("d_E", ei)])
                for cp in range(2):
                    c.op("pe", lambda cp=cp: T.matmul(acc[cp][:], lhsT=Va[:, kb, h, :], rhs=Et[ei][:, cp, :], start=(kb == 0), stop=(kb == nkb - 1)),
                         r=["d_Va0", "d_Va1", ("d_E", ei)], w=[("d_acc", cp)])
            for cp in range(2):
                c.op("act", lambda cp=cp: A.activation(out=rc[cp][:], in_=acc[cp][64:128, :], func=AF.Ln), r=[("d_acc", cp)], w=[("d_rc", cp)])
                c.op("act", lambda cp=cp: A.activation(out=rc[cp][:], in_=rc[cp][:], func=AF.Exp, scale=-1.0), r=[("d_rc", cp)], w=[("d_rc", cp)])
                c.op("dve", lambda cp=cp: V.tensor_tensor(out=oo[cp][:], in0=acc[cp][0:64, :], in1=rc[cp][:], op=ALU.mult), r=[("d_acc", cp), ("d_rc", cp)], w=[("d_o", cp)])
            if qi + 1 < len(QH):
                prologue(*QH[qi + 1])
            c.op("dve", lambda: V.scalar_tensor_tensor(out=od[:], in0=oo[1][:], scalar=nlamc[0:64, 0:1], in1=oo[0][:], op0=ALU.mult, op1=ALU.add),
                 r=[("d_o", 0), ("d_o", 1), "d_nlamc"], w=["d_od"])
            c.op("act", lambda: A.activation(out=sqd[:], in_=od[:], func=AF.Square), r=["d_od"], w=["d_sqd"])
            c.op("pe", lambda: T.matmul(ssq[0:64, :], lhsT=cst[0:64, 2, 0:64], rhs=sqd[:], start=True, stop=True), r=["cst", "d_sqd"], w=[SSQT])
            c.op("act", lambda: A.activation(out=rsd[:], in_=ssq[0:64, :], func=AF.Ln, scale=1.0 / 64, bias=e.epsc[0:64, 0:1]), r=[SSQT, "epsc"], w=["d_rsd"])
            c.op("act", lambda: A.activation(out=rsd[:], in_=rsd[:], func=AF.Exp, scale=-0.5), r=["d_rsd"], w=["d_rsd"])
            fi = fin % 2
            fin += 1
            c.op("dve", lambda: V.scalar_tensor_tensor(out=yb[fi][:], in0=od[:], scalar=gcol[0:64, 0:1], in1=rsd[:], op0=ALU.mult, op1=ALU.mult),
                 r=["d_od", "d_gcol", "d_rsd"], w=[("d_yb", fi)])
            c.dma("sp", e.mix_dst(hh, 3, 64 * h, 64, Q * 512, (Q + 1) * 512), yb[fi][:], r=[("d_yb", fi)], w=[("mixT", hh, 3)], key=("d_st", fi))


def mixer_A(nc, c, l, hh, S, env):
    e = SimpleNamespace(**env)
    V, A, G, T = nc.vector, nc.scalar, nc.gpsimd, nc.tensor
    NB = S // 128
    NSB = S // 2048
    scale = 64.0 ** -0.5
    with ExitStack() as es:
        sb = lambda n, s, d: es.enter_context(nc.sbuf_tensor(_u(n), list(s), d))
        ps = lambda n, s, d=F32: es.enter_context(nc.psum_tensor(_u(n), list(s), d))
        kT = sb("a_kT", [128, S], BF16)
        qT = sb("a_qT", [128, S], BF16)
        Va = [sb(f"a_Va{p}", [128, NB, 2, 128], BF16) for p in range(3)]
        Ra = [[sb(f"a_R{p}{h}", [128, WA], F32) for h in range(2)] for p in range(3)]
        qm = [sb(f"a_qm{h}", [128, 2048], BF16) for h in range(2)]
        Et = [sb(f"a_E{i}", [128, 256], BF16) for i in range(3)]
        tmp = [sb(f"a_tmp{i}", [128, 256], F32) for i in range(2)]
        rc = [sb(f"a_rc{i}", [64, 512], F32) for i in range(2)]
        yb = [sb(f"a_yb{i}", [64, 512], BF16) for i in range(2)]
        pall = ps("a_pall", [128, 8, 512])
        st = [pall[:, i, :] for i in range(3)]
        acc = [pall[:, 3 + i, :] for i in range(4)]

        c.dma("sp", kT[:], e.ptb_d.ap()[hh, 1], r=[("ptb", hh, 1)], w=["a_kT"], key="a_kT")
        c.dma("sp", qT[:], e.ptb_d.ap()[hh, 0], r=[("ptb", hh, 0)], w=["a_qT"], key="a_qT")
        for p, (_, dil) in enumerate(PATS):
            NBp = NB // dil
            c.op("pool", lambda p=p: G.memset(Va[p][:, :, :, 64:128], 1.0), w=[("a_Va1", p)])
            src = e.va_d.ap()[hh].rearrange("(n i r) (h d) -> r i n h d", i=128, r=dil, h=2)
            for r in range(dil):
                for h in range(2):
                    c.dma("sp", Va[p][:, r * NBp:(r + 1) * NBp, h, 0:64], src[r, :, :, h, :], r=[("va", hh)], w=[("a_Va0", p)], key="a_Va")
            for h in range(2):
                bi = (hh * 2 + h) * 3 + p
                c.dma("sp", Ra[p][h][:], bass.AP(e.banda_d, bi * 130 * WPA + 128, [[WPA, 128], [1, WA]]), r=[("banda", bi)], w=[("a_R", p, h)], key="a_R")
        for h in range(2):
            c.op("pool", lambda h=h: G.memset(qm[h][:], 0.0), w=[("a_qm", h)])

        nst = 0
        nE = 0
        ntmp = 0
        fin = 0
        LA = 2
        for sbk in range(NSB):
            T0 = sbk * 2048
            for h in range(2):
                c.op("act", lambda h=h: A.copy(out=qm[h][64 * h:64 * h + 64, :], in_=qT[64 * h:64 * h + 64, T0:T0 + 2048]), r=["a_qT"], w=[("a_qm", h)])
                started = [False] * 4
                steps = []
                for p, (_, dil) in enumerate(PATS):
                    nblk = 16 // dil
                    for r in range(dil):
                        for i in range(nblk):
                            steps.append((p, dil, r, i))

                def qk(step, si):
                    p, dil, r, i = step
                    nblk = 16 // dil
                    n = sbk * nblk + i
                    qs = slice(i * 128 * dil + r, (i + 1) * 128 * dil, dil)
                    ks = lambda nn: slice(nn * 128 * dil + r, (nn + 1) * 128 * dil, dil)
                    c.op("pe", lambda: T.matmul(st[si][:, 0:128], lhsT=kT[:, ks(n)], rhs=qm[h][:, qs], start=True, stop=True),
                         r=["a_kT", ("a_qm", h)], w=[("a_st", si)], sig=(n == 0))
                    if n >= 1:
                        c.op("pe", lambda: T.matmul(st[si][:, 128:256], lhsT=kT[:, ks(n - 1)], rhs=qm[h][:, qs], start=True, stop=True),
                             r=["a_kT", ("a_qm", h)], w=[("a_st", si)])

                for k in range(min(LA, len(steps))):
                    qk(steps[k], (nst + k) % 3)
                for idx, (p, dil, r, i) in enumerate(steps):
                    NBp = NB // dil
                    nblk = 16 // dil
                    n = sbk * nblk + i
                    si = nst % 3
                    nst += 1
                    if idx + LA < len(steps):
                        qk(steps[idx + LA], (si + LA) % 3)
                    W = 256 if n >= 1 else 128
                    ti = ntmp % 2
                    ntmp += 1
                    ei = nE % 3
                    nE += 1
                    c.op("dve", lambda: V.scalar_tensor_tensor(out=tmp[ti][:, 0:W], in0=st[si][:, 0:W], scalar=scale, in1=Ra[p][h][:, 0:W], op0=ALU.mult, op1=ALU.add),
                         r=[("a_st", si), ("a_R", p, h)], w=[("a_tmp", ti)])
                    c.op("act", lambda: A.activation(out=Et[ei][:, 0:W], in_=tmp[ti][:, 0:W], func=AF.Exp), r=[("a_tmp", ti)], w=[("a_E", ei)])
                    last = (p == 2 and r == dil - 1)
                    for part in range(2 if n >= 1 else 1):
                        nn = n - part
                        lhs = Va[p][:, r * NBp + nn, h, :]
                        if dil == 1:
                            segs = [((i * 128) // 512, slice((i % 4) * 128, (i % 4) * 128 + 128), slice(part * 128, part * 128 + 128))]
                        elif dil == 4:
                            segs = [(i, slice(r, 512, 4), slice(part * 128, part * 128 + 128))]
                        else:
                            segs = [(j, slice(r, 512, 16), slice(part * 128 + 32 * j, part * 128 + 32 * j + 32)) for j in range(4)]
                        for (bj, osl, esl) in segs:
                            stt = not started[bj]
                            started[bj] = True
                            c.op("pe", lambda: T.matmul(acc[bj][:, osl], lhsT=lhs, rhs=Et[ei][:, esl], start=stt, stop=last, skip_group_check=True),
                                 r=[("a_Va0", p), ("a_Va1", p), ("a_E", ei)], w=[("a_acc", bj)])
                for j in range(4):
                    fi = fin % 2
                    fin += 1
                    c.op("act", lambda: A.activation(out=rc[fi][:], in_=acc[j][64:128, :], func=AF.Ln), r=[("a_acc", j)], w=[("a_rc", fi)])
                    c.op("act", lambda: A.activation(out=rc[fi][:], in_=rc[fi][:], func=AF.Exp, scale=-1.0), r=[("a_rc", fi)], w=[("a_rc", fi)])
                    c.op("dve", lambda: V.tensor_tensor(out=yb[fi][:], in0=acc[j][0:64, :], in1=rc[fi][:], op=ALU.mult), r=[("a_acc", j), ("a_rc", fi)], w=[("a_yb", fi)])
                    c.dma("sp", e.mix_dst(hh, 0, 64 * h, 64, T0 + j * 512, T0 + (j + 1) * 512), yb[fi][:], r=[("a_yb", fi)], w=[("mixT", hh, 0)], key=("a_st", fi))


def phase_p3(nc, c, l, S, L, env):
    e = SimpleNamespace(**env)
    V, A, G, T = nc.vector, nc.scalar, nc.gpsimd, nc.tensor
    NQ = e.NQL
    MC = e.MC
    pair = e.pair
    last_layer = (l == L - 1)
    h_src = e.x_d if l == 0 else e.hbuf_d
    with ExitStack() as es:
        sb = lambda n, s, d: es.enter_context(nc.sbuf_tensor(_u(n), list(s), d))
        ps = lambda n, s, d=F32: es.enter_context(nc.psum_tensor(_u(n), list(s), d))
        wout = sb("f_wout", [128, MC, 1024], BF16)
        wdn = sb("f_wdn", [128, 32, 1024], BF16)
        gt = sb("f_gt", [128, 4, 1024], F32)
        mt = [sb(f"f_mt{i}", [128, MC, 512], BF16) for i in range(2 if pair else 1)]
        rsel = sb("f_rsel", [128, 2], F32)
        c.dma("sp", rsel[:], e.rsel_d.ap(), w=["f_rsel"], key="f_rsel")
        u2T = sb("f_u2T", [128, 8, 512], BF16)
        fT = sb("f_fT", [128, 32, 512], BF16)
        wupt = [sb(f"f_wup{i}", [128, 8, 256], BF16) for i in range(2)]
        ht = [sb(f"f_ht{i}", [128, 1024], F32) for i in range(1)]
        hmid = sb("f_hmid", [128, 4, 1024], F32)
        hnew = [sb(f"f_hnew{i}", [128, 1024], F32) for i in range(1)]
        ub = [sb(f"f_ub{i}", [128, 1024], BF16) for i in range(2)]
        sqv = [sb(f"f_sqv{i}", [128, 512], F32) for i in range(2)]
        ssq = sb("f_ssq", [128, 4], F32)
        ssp = sb("f_ssp", [128, 4], F32)
        uTs = [sb(f"f_uTs{i}", [128, 8, 512], BF16) for i in range(1)]
        po = [[ps(f"f_po{i}{k}", [128, 512]) for k in range(2)] for i in range(2)]
        pu = [ps(f"f_pu{i}", [128, 512]) for i in range(2)]
        pst = [ps(f"f_pst{i}", [128, 1024], BF16) for i in range(2)]

        c.dma("sp", wout[:], e.woutb_d.ap()[l].rearrange("(c p) d -> p c d", p=128), r=[("woutb", l)], w=["f_wout"], key="f_wout")
        c.dma("sp", wdn[:], e.wdnb_d.ap()[l].rearrange("(f p) d -> p f d", p=128), r=[("wdnb", l)], w=["f_wdn"], key="f_wdn")
        for k in range(1, 4):
            c.dma("sp", gt[:, k, :], bass.AP(e.gains_d, (l * 4 + k) * D_MODEL, [[0, 128], [1, D_MODEL]]), w=[("f_gt", k)], key="f_gt")
        if not last_layer:
            c.dma("sp", gt[:, 0, :], bass.AP(e.gains_d, ((l + 1) * 4) * D_MODEL, [[0, 128], [1, D_MODEL]]), w=[("f_gt", 0)], key="f_gt")

        def post_norm_res(pacc, ptags, gk, res_ap, res_tag, out_ap, out_tag):
            for k in range(2):
                c.op("act", lambda k=k: A.activation(out=sqv[k][:], in_=pacc[k][:], func=AF.Square, accum_out=ssp[:, k:k + 1]),
                     r=[ptags[k]], w=[("f_sqv", k), ("f_ssp", k)])
            c.op("dve", lambda: V.tensor_tensor(out=ssp[:, 2:3], in0=ssp[:, 0:1], in1=ssp[:, 1:2], op=ALU.add), r=[("f_ssp", 0), ("f_ssp", 1)], w=["f_ssp2"])
            rstd_from_ss_local(ssp[:, 2:3], ssp[:, 3:4], "f_ssp2", "f_ssp3")
            for k in range(2):
                c.op("dve", lambda k=k: V.scalar_tensor_tensor(out=out_ap[:, k * 512:(k + 1) * 512], in0=pacc[k][:], scalar=ssp[:, 3:4], in1=gt[:, gk, k * 512:(k + 1) * 512], op0=ALU.mult, op1=ALU.mult),
                     r=[ptags[k], "f_ssp3", ("f_gt", gk)], w=[out_tag])
            c.op("pool", lambda: G.tensor_tensor(out=out_ap, in0=out_ap, in1=res_ap, op=ALU.add), r=[out_tag, res_tag], w=[out_tag])

        tmpc = sb("f_tmpc", [128, 2], F32)

        def rstd_from_ss_local(ss_ap, out_ap, tin, tout):
            c.op("act", lambda: A.activation(out=tmpc[:, 0:1], in_=ss_ap, func=AF.Sqrt, scale=1.0 / D_MODEL, bias=e.epsc[:, 0:1]), r=[tin, "epsc"], w=["f_tmpc"])
            c.op("dve", lambda: V.reciprocal(out=out_ap, in_=tmpc[:, 0:1]), r=["f_tmpc"], w=[tout])

        def load_mix(q_):
            tk = q_ * 512
            m_ = mt[0]
            if not pair:
                c.dma("sp", m_[:], e.mixT_d[0].ap()[:, tk:tk + 512].rearrange("(c p) t -> p c t", p=128),
                      r=[("mixT", hh_, k_) for hh_ in range(e.NHH) for k_ in range(4)], w=[("f_mt", 0)], key=("f_mt", 0))
            else:
                for mm_ in range(4):
                    c.dma("sp", m_[:, mm_:8:4, :], e.mixg_d[mm_].ap()[:, tk:tk + 512].rearrange("(r p) t -> p r t", p=128), r=[("mixg", mm_)], w=[("f_mt", 0)], key=("f_mt", 0))
                    c.dma("sp", mt[1][:, mm_:8:4, :], e.mixg_d[mm_].ap()[:, e.SL + tk:e.SL + tk + 512].rearrange("(r p) t -> p r t", p=128), r=[("mixg", mm_)], w=[("f_mt", 1)], key=("f_mt", 1))

        nt = 0
        npo = 0
        npu = 0
        nw = 0
        for q in range(NQ):
            tok0 = q * 512
            m = mt[0]
            mtag = ("f_mt", 0)
            if q == 0:
                load_mix(0)
            pas = {}

            def outproj(j):
                nonlocal npo
                if pair:
                    js = slice(j * 128, (j + 1) * 128)
                    c.op("pool", lambda: G.tensor_scalar(out=m[:, :, js], in0=m[:, :, js], scalar1=rsel[:, 0:1], scalar2=0.0, op0=ALU.mult, op1=ALU.add), r=[mtag, "f_rsel"], w=[mtag])
                    c.op("dve", lambda: V.scalar_tensor_tensor(out=m[:, :, js], in0=mt[1][:, :, js], scalar=rsel[:, 1:2], in1=m[:, :, js], op0=ALU.mult, op1=ALU.add), r=[mtag, ("f_mt", 1), "f_rsel"], w=[mtag])
                pa = po[npo % 2]
                ptags = [("f_po", npo % 2, 0), ("f_po", npo % 2, 1)]
                npo += 1
                for k in range(2):
                    for cc in range(MC):
                        c.op("pe", lambda k=k, cc=cc: T.matmul(pa[k][:], lhsT=m[:, cc, j * 128:(j + 1) * 128], rhs=wout[:, cc, k * 512:(k + 1) * 512], start=(cc == 0), stop=(cc == MC - 1)),
                             r=[mtag, "f_wout"], w=[ptags[k]], sig=(cc == MC - 1))
                pas[j] = (pa, ptags)

            def post1(j):
                pa, ptags = pas[j]
                c.dma("sp", ht[0][:], h_src.ap()[tok0 + j * 128:tok0 + (j + 1) * 128, :], r=[("hbuf", q)] if l > 0 else [], w=[("f_ht", 0)], key=("f_ht", 0))
                post_norm_res(pa, ptags, 1, ht[0][:], ("f_ht", 0), hmid[:, j, :], ("f_hmid", j))
                e.norm_to_uT(hmid[:, j, :], ("f_hmid", j), gt[:, 2, :], ("f_gt", 2), ssq, ub, pst, u2T, "f_u2T", j, 0)

            outproj(0)
            for j in range(4):
                if j + 1 < 4:
                    outproj(j + 1)
                elif q + 1 < NQ:
                    load_mix(q + 1)
                post1(j)
            for fg in range(16):
                wt = wupt[nw % 2]
                wtag = ("f_wup", nw % 2)
                nw += 1
                c.dma("sp", wt[:], e.wupb_d.ap()[l][:, fg * 256:(fg + 1) * 256].rearrange("(c p) f -> p c f", p=128), r=[("wupb", l)], w=[wtag], key=wtag)
                for fc in range(2):
                    f = fg * 2 + fc
                    pi = npu % 2
                    npu += 1
                    for cc in range(8):
                        c.op("pe", lambda cc=cc: T.matmul(pu[pi][:], lhsT=wt[:, cc, fc * 128:(fc + 1) * 128], rhs=u2T[:, cc, :], start=(cc == 0), stop=(cc == 7)),
                             r=[wtag, "f_u2T"], w=[("f_pu", pi)], sig=(cc == 7))
                    c.op("act", lambda: A.activation(out=sqv[pi][:], in_=pu[pi][:], func=AF.Square), r=[("f_pu", pi)], w=[("f_sqv", pi)])
                    c.op("dve", lambda: V.scalar_tensor_tensor(out=fT[:, f, :], in0=pu[pi][:], scalar=0.0, in1=sqv[pi][:], op0=ALU.is_gt, op1=ALU.mult),
                         r=[("f_pu", pi), ("f_sqv", pi)], w=[("f_fT", f)])
            pbs = {}

            def down(j):
                nonlocal npo
                pa = po[npo % 2]
                ptags = [("f_po", npo % 2, 0), ("f_po", npo % 2, 1)]
                npo += 1
                for k in range(2):
                    for f in range(32):
                        c.op("pe", lambda k=k, f=f: T.matmul(pa[k][:], lhsT=fT[:, f, j * 128:(j + 1) * 128], rhs=wdn[:, f, k * 512:(k + 1) * 512], start=(f == 0), stop=(f == 31)),
                             r=[("f_fT", f), "f_wdn"], w=[ptags[k]], sig=(f == 31))
                pbs[j] = (pa, ptags)

            def post2(j):
                pa, ptags = pbs[j]
                hn = hnew[0]
                hntag = ("f_hnew", 0)
                post_norm_res(pa, ptags, 3, hmid[:, j, :], ("f_hmid", j), hn[:], hntag)
                dst = e.out_d if last_layer else e.hbuf_d
                c.dma("act", dst.ap()[tok0 + j * 128:tok0 + (j + 1) * 128, :], hn[:], r=[hntag], w=[("hbuf", q)] if not last_layer else [("outd", q)], key=("f_sth", j % 2))
                if not last_layer:
                    e.norm_to_uT(hn[:], hntag, gt[:, 0, :], ("f_gt", 0), ssq, ub, pst, uTs[0], ("f_uTs", 0), j, 1)

            down(0)
            for j in range(4):
                if j + 1 < 4:
                    down(j + 1)
                post2(j)
            if not last_layer:
                e.uT_store(q, uTs[0], ("f_uTs", 0), ("f_stu", 0))


def _consts():
    cst = np.zeros((128, 8, 128), np.float32)
    cst[:, 0, :] = np.eye(128, dtype=np.float32)
    s_ = np.arange(128)[:, None]
    l_ = np.arange(128)[None, :]
    cst[:, 1, :] = np.where(l_ >= s_, 0.0, NEG).astype(np.float32)
    cst[:, 2, :] = 1.0
    for i, r in enumerate((0, 1, 32, 33)):
        cst[r, 3, i] = 1.0
    cst[32, 4, :] = 1.0
    cst[33, 5, :] = 1.0
    return cst


def prep_core_inputs(inp, b, hhs, S=None, L=None, rank=0, tok_lo=0, tok_hi=None):
    f32 = np.float32
    L = L or inp["w_in"].shape[0]
    S = S or inp["x"].shape[1]
    NHH = len(hhs)
    cols_fm = []
    for hh in hhs:
        for base in (0, 256, 768, 1024, 1280, 1536, 1792, 2048, 2308, 2564):
            cols_fm.append(np.arange(base + 128 * hh, base + 128 * hh + 128))
    cols_tm = []
    for hh in hhs:
        cols_tm.append(np.arange(512 + 128 * hh, 512 + 128 * hh + 128))
        cols_tm.append(np.arange(2820 + 128 * hh, 2820 + 128 * hh + 128))
    cols_dt = [np.arange(2304 + 2 * hh, 2304 + 2 * hh + 2) for hh in hhs]
    cols = np.concatenate(cols_fm + cols_tm + cols_dt)
    w_in = np.ascontiguousarray(inp["w_in"][:L][:, :, cols]).astype(f32)
    rows = np.concatenate([np.arange(m * 256 + hh * 128, m * 256 + hh * 128 + 128) for hh in (0, 1) for m in range(4)])
    w_out = np.ascontiguousarray(inp["w_out"][:L][:, rows, :]).astype(f32)
    gains = np.stack([inp["norm_mix_pre"][:L], inp["norm_mix_post"][:L], inp["norm_ffn_pre"][:L], inp["norm_ffn_post"][:L]], axis=1).astype(f32)
    pp = np.zeros((L, NHH, 128, NPAR), f32)
    lruw = np.zeros((L, NHH, 2, 128, 128), f32)
    for i, hh in enumerate(hhs):
        ch = slice(128 * hh, 128 * hh + 128)
        pp[:, i, :, 0:4] = np.transpose(inp["lru_conv_w"][:L][:, :, ch], (0, 2, 1))
        pp[:, i, :, 4] = inp["lru_conv_b"][:L][:, ch]
        pp[:, i, :, 5] = inp["lru_ba"][:L][:, ch]
        pp[:, i, :, 6] = inp["lru_bx"][:L][:, ch]
        pp[:, i, :, 7] = inp["lru_lambda"][:L][:, ch]
        for k, off in enumerate((0, 256, 512)):
            cs_ = slice(off + 128 * hh, off + 128 * hh + 128)
            pp[:, i, :, 8 + 5 * k:12 + 5 * k] = np.transpose(inp["ssm_conv_w"][:L][:, :, cs_], (0, 2, 1))
            pp[:, i, :, 12 + 5 * k] = inp["ssm_conv_b"][:L][:, cs_]
        pp[:, i, :, 23] = inp["ssm_norm"][:L][:, ch]
        pp[:, i, :, 24] = np.repeat(inp["ssm_d"][:L][:, 2 * hh:2 * hh + 2], 64, axis=1)
        pp[:, i, :, 25] = np.tile(inp["diff_norm"][:L], (1, 2))
        for r in range(2):
            for row in (r, 32 + r):
                pp[:, i, row, 26] = inp["ssm_dt_bias"][:L][:, 2 * hh + r]
                pp[:, i, row, 27] = inp["ssm_a_log"][:L][:, 2 * hh + r]
        for k, nm in enumerate(("lru_wa", "lru_wx")):
            for r in range(2):
                lruw[:, i, k, 64 * r:64 * r + 64, 64 * r:64 * r + 64] = inp[nm][:L][:, 2 * hh + r]
    lam4 = np.stack([inp["diff_lq1"][:L], inp["diff_lk1"][:L], inp["diff_lq2"][:L], inp["diff_lk2"][:L]], axis=1).astype(f32)
    rel = np.asarray(inp["rel_bias"]).astype(f32)
    dist = np.arange(LVD) - 511
    bidx = t5_bucket_np(dist)
    bvd = np.zeros((NHH * 2, LVD), f32)
    bva = np.zeros((NHH * 2 * 3, LVA), f32)
    relv = np.arange(LVA) - 127
    for i, hh in enumerate(hhs):
        for h in range(2):
            head = 2 * hh + h
            bvd[i * 2 + h] = np.where(dist >= 0, rel[bidx, 4 + head], f32(NEG))
            for p, (_, dil) in enumerate(PATS):
                ok = (relv >= 0) & (relv <= 128)
                bva[(i * 2 + h) * 3 + p] = np.where(ok, rel[t5_bucket_np(relv * dil), head], f32(NEG))
    rsel = np.zeros((128, 2), f32)
    rsel[:, rank] = 1.0
    return {
        "x": np.ascontiguousarray(inp["x"][b, :S][tok_lo:tok_hi]).astype(f32), "rsel": rsel,
        "w_in": w_in, "w_out": w_out,
        "w_up": np.ascontiguousarray(inp["w_ff_up"][:L]).astype(f32),
        "w_dn": np.ascontiguousarray(inp["w_ff_down"][:L]).astype(f32),
        "gains": gains, "pp": pp, "lruw": lruw, "lam4": lam4, "bvd": bvd, "bva": bva, "cst": _consts(),
    }


_NC_CACHE = {}


def kernel(**inputs):
    inp = {k: np.asarray(v) for k, v in inputs.items()}
    B, S, _ = inp["x"].shape
    L = inp["w_in"].shape[0]
    key = (S, L)
    if key not in _NC_CACHE:
        _NC_CACHE[key] = build(S=S, NHH=1, L=L, pair=True)[0]
    nc = _NC_CACHE[key]
    H = S // 2
    in_maps = [prep_core_inputs(inp, i // 2, [i % 2], rank=i % 2, tok_lo=(i % 2) * H, tok_hi=(i % 2 + 1) * H) for i in range(8)]
    res = run_bass_kernel_spmd(nc, in_maps, core_ids=list(range(8)))
    out = np.empty((B, S, D_MODEL), np.float32)
    for i in range(8):
        out[i // 2, (i % 2) * H:(i % 2 + 1) * H] = res.results[i]["out"]
    return out
```

```python
import math
from contextlib import ExitStack
from types import SimpleNamespace

import numpy as np
import concourse.bass as bass
import concourse.mybir as mybir
from concourse.bass_utils import run_bass_kernel_spmd

F32 = mybir.dt.float32
BF16 = mybir.dt.bfloat16
AF = mybir.ActivationFunctionType
ALU = mybir.AluOpType
AX = mybir.AxisListType

SEM_CAP = 30000
EMBED_WAITS = True
_UID = [0]


def _u(n):
    _UID[0] += 1
    return f"{n}_{_UID[0]}"


class Ev:
    __slots__ = ("sem", "val", "eng", "dkey")

    def __init__(self):
        self.sem = None
        self.val = None
        self.eng = None
        self.dkey = None


class Ctx:
    def __init__(self, nc):
        self.nc = nc
        self.eng = {"pe": nc.tensor, "act": nc.scalar, "dve": nc.vector, "pool": nc.gpsimd, "sp": nc.sync}
        self.esems = {e: [] for e in ("pe", "act", "dve", "pool")}
        self.ecnt = {e: 0 for e in self.esems}
        self.waited = {e: {} for e in self.eng}
        self.lastw = {}
        self.reads = {}
        self.dsem = {}
        self.dcnt = {}
        self.pending_pe = []
        self.nsem = 0
        self.n_ops = 0
        self.n_waits = 0

    def _newsem(self, name):
        self.nsem += 1
        return self.nc.alloc_semaphore(name=f"{name}_{self.nsem}")

    def _deps(self, r, w):
        deps = []
        for t in r:
            e = self.lastw.get(t)
            if e is not None:
                deps.append(e)
        for t in w:
            e = self.lastw.get(t)
            if e is not None:
                deps.append(e)
            deps.extend(self.reads.get(t, {}).values())
        return deps

    def _wait(self, eng, deps, embed=False):
        wd = self.waited[eng]
        need = {}
        for ev in deps:
            if ev.eng == "pe" and eng == "pe":
                continue
            assert ev.sem is not None, "dependency on a PE op that has not signalled yet (mark sig=True)"
            val = ev.val
            if ev.dkey is not None:
                val = max(val, 16 * self.dcnt[ev.dkey])
            k = ev.sem.name
            if wd.get(k, 0) >= val:
                continue
            if k not in need or need[k][1] < val:
                need[k] = (ev.sem, val)
        items = list(need.items())
        emb = None
        if embed and items:
            emb = items.pop()
        for k, (sem, val) in items:
            self.eng[eng].wait_ge(sem, val)
            wd[k] = val
            self.n_waits += 1
        if emb is not None:
            wd[emb[0]] = emb[1][1]
            return emb[1]
        return None

    def _record(self, ev, r, w):
        rk = ev.eng if ev.dkey is None else ("dma", ev.dkey)
        for t in w:
            self.lastw[t] = ev
            self.reads[t] = {}
        for t in r:
            self.reads.setdefault(t, {})[rk] = ev

    def op(self, eng, fn, r=(), w=(), sig=True):
        emb = self._wait(eng, self._deps(r, w), embed=EMBED_WAITS)
        ins = fn()
        if emb is not None:
            ins._wait_ge(emb[0], emb[1])
        self.n_ops += 1
        ev = Ev()
        ev.eng = eng
        if sig:
            k = self.ecnt[eng]
            si, v = divmod(k, SEM_CAP)
            if si >= len(self.esems[eng]):
                self.esems[eng].append(self._newsem(f"e_{eng}"))
            sem = self.esems[eng][si]
            ins.then_inc(sem, 1)
            self.ecnt[eng] = k + 1
            ev.sem, ev.val = sem, v + 1
            if eng == "pe":
                for p in self.pending_pe:
                    p.sem, p.val = sem, v + 1
                self.pending_pe = []
        else:
            assert eng == "pe"
            self.pending_pe.append(ev)
        self._record(ev, r, w)
        return ins

    def dma(self, q, out, in_, r=(), w=(), key=None, **kw):
        assert key is not None
        self._wait(q, self._deps(r, w))
        if key not in self.dsem:
            self.dsem[key] = self._newsem("d")
            self.dcnt[key] = 0
        ins = self.eng[q].dma_start(out=out, in_=in_, **kw)
        self.dcnt[key] += 1
        ins.then_inc(self.dsem[key], 16)
        ev = Ev()
        ev.sem, ev.val, ev.eng, ev.dkey = self.dsem[key], 16 * self.dcnt[key], "dma", key
        self._record(ev, r, w)
        self.n_ops += 1
        return ins

    def coll(self, kind, in_ap, out_ap, groups, r=(), w=()):
        self._wait("pool", self._deps(r, w))
        if not hasattr(self, "csem"):
            self.csem = self._newsem("cc")
            self.ccnt = 0
        ins = self.nc.gpsimd.collective_compute(kind, mybir.AluOpType.bypass, replica_groups=groups,
                                                ins=[in_ap.opt()], outs=[out_ap.opt()])
        ins.then_inc(self.csem)
        self.ccnt += 1
        ev = Ev()
        ev.sem, ev.val, ev.eng = self.csem, self.ccnt, "cc"
        self._record(ev, r, w)
        self.n_ops += 1
        return ins

    def finish(self, q="sp"):
        deps = []
        for key, sem in self.dsem.items():
            ev = Ev()
            ev.sem, ev.val, ev.eng, ev.dkey = sem, 16 * self.dcnt[key], "dma", key
            deps.append(ev)
        for e, sems in self.esems.items():
            if sems:
                ev = Ev()
                k = self.ecnt[e]
                si, v = divmod(k - 1, SEM_CAP)
                ev.sem, ev.val, ev.eng = sems[si], v + 1, e
                deps.append(ev)
        self._wait(q, deps)


D_MODEL = 1024
D_FF = 4096
EPS = 1e-6
NEG = -30000.0
NPAR = 28
WD = 2432
LVD = WD + 127
WPD = WD + 128
WA = 256
LVA = WA + 127
WPA = WA + 128
FAR_D = 1664
PATS = ((128, 1), (512, 4), (2048, 16))


def t5_bucket_np(n):
    n = np.maximum(n, 0)
    nf = np.maximum(n, 1).astype(np.float32)
    large = 16 + (np.log(nf / np.float32(16)) / np.float32(math.log(2048 / 16)) * np.float32(16)).astype(np.int32)
    large = np.minimum(large, 31)
    return np.where(n < 16, n, large)


def build(S=8192, NHH=2, L=2, debug=False, phases=("p0", "p1", "A", "B", "C", "D", "p3"), pair=False):
    nc = bass.Bass("TRN2", target_bir_lowering=False)
    NB = S // 128
    NQ = S // 512
    NCOL = NHH * 1538
    NMIXL = NHH * 512
    NMIX = 1024
    MC = NMIX // 128
    SL = S // 2 if pair else S
    NQL = SL // 512
    GROUPS = [[0, 1], [2, 3], [4, 5], [6, 7]]
    assert (pair and NHH == 1) or (not pair and NHH == 2)
    c = Ctx(nc)

    def din(name, shape, dt=F32):
        return nc.dram_tensor(name, list(shape), dt, kind="ExternalInput")

    def dscr(name, shape, dt):
        return nc.dram_tensor(name, list(shape), dt, kind="ExternalOutput" if debug else "Internal")

    x_d = din("x", [SL, D_MODEL])
    rsel_d = din("rsel", [128, 2])
    win_d = din("w_in", [L, D_MODEL, NCOL])
    wout_d = din("w_out", [L, NMIX, D_MODEL])
    wup_d = din("w_up", [L, D_MODEL, D_FF])
    wdn_d = din("w_dn", [L, D_FF, D_MODEL])
    gains_d = din("gains", [L, 4, D_MODEL])
    pp_d = din("pp", [L, NHH, 128, NPAR])
    lruw_d = din("lruw", [L, NHH, 2, 128, 128])
    lam4_d = din("lam4", [L, 4, 32])
    bvd_d = din("bvd", [NHH * 2, LVD])
    bva_d = din("bva", [NHH * 2 * 3, LVA])
    cst_d = din("cst", [128, 8, 128])
    out_d = nc.dram_tensor("out", [SL, D_MODEL], F32, kind="ExternalOutput")

    TOKC = 1024 if pair else S
    NUK = SL // TOKC
    if pair:
        uTl_d = [nc.dram_tensor(f"uTl{k}", [D_MODEL, TOKC], BF16) for k in range(NUK)]
        uTg_d = [nc.dram_tensor(f"uTg{k}", [2 * D_MODEL, TOKC], BF16) for k in range(NUK)]
    else:
        uTl_d = [dscr("uT", [D_MODEL, S], BF16)]
        uTg_d = uTl_d
    ptb_d = dscr("ptb", [NHH, 4, 128, S], BF16)
    ptf_d = dscr("ptf", [NHH, 6, 128, S], F32)
    dtT_d = dscr("dtT", [NHH, 2, S], F32)
    va_d = dscr("va", [NHH, S, 128], BF16)
    vd_d = dscr("vd", [NHH, S, 128], BF16)
    if pair:
        mixT_d = [nc.dram_tensor(f"mixT{m}", [128, S], BF16) for m in range(4)]
        mixg_d = [nc.dram_tensor(f"mixg{m}", [256, S], BF16) for m in range(4)]
    else:
        mixT_d = [dscr("mixT", [NMIXL, S], BF16)]
        mixg_d = mixT_d

    def mix_dst(hh, m, r0, nrows, t0, t1):
        if pair:
            return mixT_d[m].ap()[r0:r0 + nrows, t0:t1]
        row0 = (hh * 4 + m) * 128 + r0
        return mixT_d[0].ap()[row0:row0 + nrows, t0:t1]

    def uT_store(q, uts_tile, rtag, key):
        k, off = divmod(q * 512, TOKC)
        c.dma("act", uTl_d[k].ap()[:, off:off + 512].rearrange("(c p) t -> p c t", p=128), uts_tile[:], r=[rtag], w=[("uTl", q)], key=key)
        if pair and (off + 512 == TOKC):
            qs_ = [k * (TOKC // 512) + i for i in range(TOKC // 512)]
            c.coll("AllGather", uTl_d[k].ap(), uTg_d[k].ap(), GROUPS, r=[("uTl", q_) for q_ in qs_],
                   w=[("uTg", rk_ * NQL + q_) for rk_ in range(2) for q_ in qs_])

    hbuf_d = dscr("hbuf", [SL, D_MODEL], F32)
    winb_d = nc.dram_tensor("winb", [L, D_MODEL, NCOL], BF16, kind="Internal")
    woutb_d = nc.dram_tensor("woutb", [L, NMIX, D_MODEL], BF16, kind="Internal")
    wupb_d = nc.dram_tensor("wupb", [L, D_MODEL, D_FF], BF16, kind="Internal")
    wdnb_d = nc.dram_tensor("wdnb", [L, D_FF, D_MODEL], BF16, kind="Internal")
    bandd_d = nc.dram_tensor("bandd", [NHH * 2, 130 * WPD], F32, kind="Internal")
    banda_d = nc.dram_tensor("banda", [NHH * 2 * 3, 130 * WPA], F32, kind="Internal")

    V = nc.vector
    A = nc.scalar
    G = nc.gpsimd
    T = nc.tensor

    def barrier():
        assert not c.pending_pe
        evs = []
        for key, sem in c.dsem.items():
            ev = Ev()
            ev.sem, ev.val, ev.eng, ev.dkey = sem, 16 * c.dcnt[key], "dma", key
            evs.append(ev)
        for e, sems in c.esems.items():
            if sems:
                ev = Ev()
                k = c.ecnt[e]
                si, v = divmod(k - 1, SEM_CAP)
                ev.sem, ev.val, ev.eng = sems[si], v + 1, "x" + e
                evs.append(ev)
        for e in ("pe", "act", "dve", "pool", "sp"):
            c._wait(e, evs)

    gs = ExitStack()

    def sbp(name, shape, dt):
        return gs.enter_context(nc.sbuf_tensor(_u(name), list(shape), dt))

    cst = sbp("cst", [128, 8, 128], F32)
    ident_b = sbp("ident_b", [128, 128], BF16)
    ones_b = sbp("ones_b", [128, 64], BF16)
    c.dma("sp", cst[:], cst_d.ap(), w=["cst"], key="cst")
    c.op("dve", lambda: V.tensor_copy(out=ident_b[:], in_=cst[:, 0, :]), r=["cst"], w=["ident_b"])
    c.op("dve", lambda: V.tensor_copy(out=ones_b[:], in_=cst[:, 2, 0:64]), r=["cst"], w=["ones_b"])
    ident_f = cst[:, 0, :]
    trimask = cst[:, 1, :]
    ones_f = cst[:, 2, :]

    epsc = sbp("epsc", [128, 2], F32)
    c.op("dve", lambda: V.memset(epsc[:, 0:1], EPS), w=["epsc"])
    c.op("dve", lambda: V.memset(epsc[:, 1:2], 1.0), w=["epsc"])

    def cast_in(l):
        for i in range(8):
            for a0 in range(0, NCOL, 2048):
                a1 = min(NCOL, a0 + 2048)
                c.dma("pool", winb_d.ap()[l, i * 128:(i + 1) * 128, a0:a1], win_d.ap()[l, i * 128:(i + 1) * 128, a0:a1], w=[("winb", l)], key=("winb", l))

    def cast_out(l):
        for i in range(NMIX // 128):
            c.dma("pool", woutb_d.ap()[l, i * 128:(i + 1) * 128, :], wout_d.ap()[l, i * 128:(i + 1) * 128, :], w=[("woutb", l)], key=("woutb", l))

    if "p1" in phases:
        cast_in(0)
    if "p3" in phases:
        cast_out(0)
    if "p3" in phases:
        for l in range(L):
            if l > 0:
                if "p1" in phases:
                    cast_in(l)
                cast_out(l)
            for i in range(8):
                c.dma("pool", wupb_d.ap()[l, i * 128:(i + 1) * 128, :].rearrange("p (a f) -> p a f", f=2048),
                      wup_d.ap()[l, i * 128:(i + 1) * 128, :].rearrange("p (a f) -> p a f", f=2048),
                      w=[("wupb", l)], key=("wupb", l))
            for i in range(32):
                c.dma("pool", wdnb_d.ap()[l, i * 128:(i + 1) * 128, :], wdn_d.ap()[l, i * 128:(i + 1) * 128, :],
                      w=[("wdnb", l)], key=("wdnb", l))

    for i in range(NHH * 2):
        dst = bass.AP(bandd_d, i * 130 * WPD + 1, [[WPD + 1, 128], [1, LVD]])
        src = bass.AP(bvd_d, i * LVD, [[0, 128], [1, LVD]])
        c.dma("sp", dst, src, w=[("bandd", i)], key="bandd")
    for i in range(NHH * 6):
        dst = bass.AP(banda_d, i * 130 * WPA + 1, [[WPA + 1, 128], [1, LVA]])
        src = bass.AP(bva_d, i * LVA, [[0, 128], [1, LVA]])
        c.dma("sp", dst, src, w=[("banda", i)], key="banda")

    def rstd_from_ss(ss_ap, out_ap, n, tag_in, tag_out, tmp_ap, tag_tmp):
        c.op("act", lambda: A.activation(out=tmp_ap, in_=ss_ap, func=AF.Sqrt, scale=1.0 / n, bias=epsc[:, 0:1]),
             r=[tag_in, "epsc"], w=[tag_tmp])
        c.op("dve", lambda: V.reciprocal(out=out_ap, in_=tmp_ap), r=[tag_tmp], w=[tag_out])


    def norm_to_uT(hsrc, htag, gt, gtag, ssq, ub, pst, uTs, uts_tag, j, k):
        jk = (j + k) % 2
        c.op("act", lambda: A.activation(out=ub[jk][:], in_=hsrc, func=AF.Square, accum_out=ssq[:, 0:1]),
             r=[htag], w=[("ub", jk), "ssq0"])
        rstd_from_ss(ssq[:, 0:1], ssq[:, 2:3], D_MODEL, "ssq0", "ssq2", ssq[:, 1:2], "ssq1")
        c.op("dve", lambda: V.scalar_tensor_tensor(out=ub[jk][:], in0=hsrc, scalar=ssq[:, 2:3], in1=gt, op0=ALU.mult, op1=ALU.mult),
             r=[htag, "ssq2", gtag], w=[("ub", jk)])
        for cc in range(8):
            c.op("pe", lambda cc=cc: T.transpose(pst[jk][:, cc * 128:(cc + 1) * 128], ub[jk][:, cc * 128:(cc + 1) * 128], ident_b[:]),
                 r=[("ub", jk), "ident_b"], w=[("pst", jk)], sig=(cc == 7))
        eng = "act" if jk == 0 else "dve"
        if eng == "act":
            c.op("act", lambda: A.copy(out=uTs[:, :, j * 128:(j + 1) * 128], in_=pst[jk][:].rearrange("p (c t) -> p c t", c=8)),
                 r=[("pst", jk)], w=[uts_tag])
        else:
            c.op("dve", lambda: V.tensor_copy(out=uTs[:, :, j * 128:(j + 1) * 128], in_=pst[jk][:].rearrange("p (c t) -> p c t", c=8)),
                 r=[("pst", jk)], w=[uts_tag])

    if "p0" in phases:
        with ExitStack() as es:
            sb = lambda n, s, d: es.enter_context(nc.sbuf_tensor(_u(n), list(s), d))
            ps = lambda n, s, d=F32: es.enter_context(nc.psum_tensor(_u(n), list(s), d))
            gt = sb("p0_g", [128, D_MODEL], F32)
            xt = [sb(f"p0_x{i}", [128, D_MODEL], F32) for i in range(2)]
            ub = [sb(f"p0_ub{i}", [128, D_MODEL], BF16) for i in range(2)]
            ssq = sb("p0_ssq", [128, 4], F32)
            uTs = [sb(f"p0_uTs{i}", [128, 8, 512], BF16) for i in range(2)]
            pst = [ps(f"p0_pst{i}", [128, 1024], BF16) for i in range(2)]
            c.dma("sp", gt[:], bass.AP(gains_d, 0, [[0, 128], [1, D_MODEL]]), w=["p0_g"], key="p0_g")
            for q in range(NQL):
                for j in range(4):
                    tb = q * 4 + j
                    c.dma("sp", xt[tb % 2][:], x_d.ap()[tb * 128:(tb + 1) * 128, :], w=[("p0_x", tb % 2)], key=("p0_x", tb % 2))
                    norm_to_uT(xt[tb % 2][:], ("p0_x", tb % 2), gt[:], "p0_g", ssq, ub, pst, uTs[q % 2], ("p0_uTs", q % 2), j, 0)
                uT_store(q, uTs[q % 2], ("p0_uTs", q % 2), ("p0_st", q % 2))
        barrier()

    for l in range(L):
        lam_init = 0.8 - 0.6 * math.exp(-0.3 * l)
        if "p1" in phases:
            with ExitStack() as es:
                sb = lambda n, s, d: es.enter_context(nc.sbuf_tensor(_u(n), list(s), d))
                ps = lambda n, s, d=F32: es.enter_context(nc.psum_tensor(_u(n), list(s), d))
                win = sb("p1_win", [128, 8, NCOL], BF16)
                c.dma("sp", win[:], winb_d.ap()[l].rearrange("(c p) n -> p c n", p=128), r=[("winb", l)], w=["p1_win"], key="p1_win")
                uTt = [sb(f"p1_uT{i}", [128, 8, 512], BF16) for i in range(2)]
                pj = [ps(f"p1_pj{i}", [128, 512]) for i in range(4)]
                stb = [sb(f"p1_stb{i}", [128, 512], BF16) for i in range(6)]
                stf = [sb(f"p1_stf{i}", [128, 512], F32) for i in range(6)]
                stv = [sb(f"p1_stv{i}", [128, NHH * 256], BF16) for i in range(2)]
                stdt = [sb(f"p1_stdt{i}", [2 * NHH, 512], F32) for i in range(2)]
                fmap = [("b", 0), ("b", 1), ("f", 0), ("f", 1), ("f", 2), ("f", 3), ("f", 4), ("f", 5), ("b", 2), ("b", 3)]
                cnt = 0
                nb_ = 0
                nf_ = 0
                nv_ = 0
                def load_u(q):
                    rk, ql = divmod(q, NQL)
                    k, off = divmod(ql * 512, TOKC)
                    c.dma("sp", uTt[q % 2][:], uTg_d[k].ap()[rk * D_MODEL:(rk + 1) * D_MODEL, off:off + 512].rearrange("(c p) t -> p c t", p=128),
                          r=[("uTg", q) if pair else ("uTl", q)], w=[("p1_uT", q % 2)], key=("p1_uT", q % 2))

                load_u(0)
                for q in range(NQ):
                    u = uTt[q % 2]
                    utag = ("p1_uT", q % 2)
                    if q + 1 < NQ:
                        load_u(q + 1)
                    for hh in range(NHH):
                        for ch in range(10):
                            col0 = (hh * 10 + ch) * 128
                            p = pj[cnt % 4]
                            ptag = ("p1_pj", cnt % 4)
                            cnt += 1
                            for cc in range(8):
                                c.op("pe", lambda cc=cc, p=p, col0=col0: T.matmul(p[:], lhsT=win[:, cc, col0:col0 + 128], rhs=u[:, cc, :], start=(cc == 0), stop=(cc == 7)),
                                     r=["p1_win", utag], w=[ptag], sig=(cc == 7))
                            kind, idx = fmap[ch]
                            if kind == "b":
                                st = stb[nb_ % 6]
                                stag = ("p1_stb", nb_ % 6)
                                nb_ += 1
                                c.op("act", lambda p=p, st=st: A.copy(out=st[:], in_=p[:]), r=[ptag], w=[stag])
                                c.dma("act", ptb_d.ap()[hh, idx, :, q * 512:(q + 1) * 512], st[:], r=[stag], w=[("ptb", hh, idx)], key=stag)
                            else:
                                st = stf[nf_ % 6]
                                stag = ("p1_stf", nf_ % 6)
                                nf_ += 1
                                c.op("dve", lambda p=p, st=st: V.tensor_copy(out=st[:], in_=p[:]), r=[ptag], w=[stag])
                                c.dma("act", ptf_d.ap()[hh, idx, :, q * 512:(q + 1) * 512], st[:], r=[stag], w=[("ptf", hh, idx)], key=stag)
                    vcol0 = NHH * 1280
                    for j in range(4):
                        p = pj[cnt % 4]
                        ptag = ("p1_pj", cnt % 4)
                        cnt += 1
                        for cc in range(8):
                            c.op("pe", lambda cc=cc, p=p, j=j: T.matmul(p[:, 0:NHH * 256], lhsT=u[:, cc, j * 128:(j + 1) * 128], rhs=win[:, cc, vcol0:vcol0 + NHH * 256], start=(cc == 0), stop=(cc == 7)),
                                 r=["p1_win", utag], w=[ptag], sig=(cc == 7))
                        st = stv[nv_ % 2]
                        stag = ("p1_stv", nv_ % 2)
                        nv_ += 1
                        c.op("act", lambda p=p, st=st: A.copy(out=st[:], in_=p[:, 0:NHH * 256]), r=[ptag], w=[stag])
                        tok0 = q * 512 + j * 128
                        for hh in range(NHH):
                            c.dma("act", va_d.ap()[hh, tok0:tok0 + 128, :], st[:, hh * 256:hh * 256 + 128], r=[stag], w=[("va", hh)], key=stag)
                            c.dma("act", vd_d.ap()[hh, tok0:tok0 + 128, :], st[:, hh * 256 + 128:hh * 256 + 256], r=[stag], w=[("vd", hh)], key=stag)
                    dcol0 = NHH * 1536
                    p = pj[cnt % 4]
                    ptag = ("p1_pj", cnt % 4)
                    cnt += 1
                    for cc in range(8):
                        c.op("pe", lambda cc=cc, p=p: T.matmul(p[0:2 * NHH, :], lhsT=win[:, cc, dcol0:dcol0 + 2 * NHH], rhs=u[:, cc, :], start=(cc == 0), stop=(cc == 7)),
                             r=["p1_win", utag], w=[ptag], sig=(cc == 7))
                    st = stdt[q % 2]
                    stag = ("p1_stdt", q % 2)
                    c.op("dve", lambda p=p, st=st: V.tensor_copy(out=st[:], in_=p[0:2 * NHH, :]), r=[ptag], w=[stag])
                    c.dma("act", dtT_d.ap()[:, :, q * 512:(q + 1) * 512].rearrange("h r t -> (h r) t"), st[:], r=[stag], w=["dtT"], key=stag)
            barrier()

        def gather_mix(m_):
            if pair:
                c.coll("AllGather", mixT_d[m_].ap(), mixg_d[m_].ap(), GROUPS, r=[("mixT", 0, m_)], w=[("mixg", m_)])

        for hh in range(NHH):
            if "B" in phases:
                mixer_B(nc, c, l, hh, S, locals())
                barrier()
                gather_mix(1)
            if "C" in phases:
                mixer_C(nc, c, l, hh, S, locals())
                barrier()
                gather_mix(2)
            if "A" in phases:
                mixer_A(nc, c, l, hh, S, locals())
                barrier()
                gather_mix(0)
            if "D" in phases:
                mixer_D(nc, c, l, hh, S, lam_init, locals())
                barrier()
                gather_mix(3)

        if "p3" in phases:
            phase_p3(nc, c, l, S, L, locals())
            barrier()

    c.finish("sp")
    gs.close()
    return nc, c


def mixer_B(nc, c, l, hh, S, env):
    e = SimpleNamespace(**env)
    V, A, G, T = nc.vector, nc.scalar, nc.gpsimd, nc.tensor
    TS = min(2048, S)
    NSEG = S // TS
    with ExitStack() as es:
        sb = lambda n, s, d: es.enter_context(nc.sbuf_tensor(_u(n), list(s), d))
        ps = lambda n, s, d=F32: es.enter_context(nc.psum_tensor(_u(n), list(s), d))
        pp = sb("b_pp", [128, NPAR], F32)
        wf = sb("b_wf", [128, 2, 128], F32)
        wb = sb("b_wb", [128, 2, 128], BF16)
        nsp = sb("b_nsp", [128, 4], F32)
        xr = sb("b_xr", [128, TS + 4], F32)
        xg = sb("b_xg", [128, TS], F32)
        xc = sb("b_xc", [128, TS], F32)
        xcb = sb("b_xcb", [128, TS], BF16)
        rr = sb("b_r", [128, TS], F32)
        ii = sb("b_i", [128, TS], F32)
        aa = sb("b_a", [128, TS], F32)
        hs = sb("b_hs", [128, TS], F32)
        yb = sb("b_yb", [128, TS], BF16)
        carry = sb("b_carry", [128, 2], F32)
        pg = [ps(f"b_pg{i}", [128, 512]) for i in range(4)]
        c.dma("sp", pp[:], e.pp_d.ap()[l, hh], w=["b_pp"], key="b_pp")
        c.dma("sp", wf[:], e.lruw_d.ap()[l, hh].rearrange("a p j -> p a j"), w=["b_wf"], key="b_wf")
        c.op("dve", lambda: V.tensor_copy(out=wb[:], in_=wf[:]), r=["b_wf"], w=["b_wb"])
        c.op("act", lambda: A.activation(out=nsp[:, 0:1], in_=pp[:, 7:8], func=AF.Exp, scale=-1.0), r=["b_pp"], w=["b_nsp0"])
        c.op("act", lambda: A.activation(out=nsp[:, 1:2], in_=nsp[:, 0:1], func=AF.Ln, bias=e.epsc[:, 1:2], scale=1.0), r=["b_nsp0", "epsc"], w=["b_nsp1"])
        c.op("dve", lambda: V.tensor_scalar(out=nsp[:, 2:3], in0=nsp[:, 1:2], scalar1=-8.0, scalar2=None, op0=ALU.mult), r=["b_nsp1"], w=["b_nsp2"])
        c.op("dve", lambda: V.tensor_scalar(out=nsp[:, 3:4], in0=nsp[:, 1:2], scalar1=-16.0, scalar2=None, op0=ALU.mult), r=["b_nsp1"], w=["b_nsp3"])
        c.op("pool", lambda: G.memset(carry[:], 0.0), w=["b_carry"])
        gate_d = e.ptf_d.ap()[hh, 0]
        xr_d = e.ptf_d.ap()[hh, 1]
        for sg in range(NSEG):
            t0 = sg * TS
            if sg == 0:
                c.op("pool", lambda: G.memset(xr[:, 0:4], 0.0), w=["b_xr"])
                c.dma("sp", xr[:, 4:4 + TS], xr_d[:, 0:TS], r=[("ptf", hh, 1)], w=["b_xr"], key="b_xr")
            else:
                c.dma("sp", xr[:, 1:4 + TS], xr_d[:, t0 - 3:t0 + TS], r=[("ptf", hh, 1)], w=["b_xr"], key="b_xr")
            c.dma("sp", xg[:], gate_d[:, t0:t0 + TS], r=[("ptf", hh, 0)], w=["b_xg"], key="b_xg")
            c.op("dve", lambda: V.tensor_scalar(out=xc[:], in0=xr[:, 1:1 + TS], scalar1=pp[:, 0:1], scalar2=pp[:, 4:5], op0=ALU.mult, op1=ALU.add),
                 r=["b_xr", "b_pp"], w=["b_xc"])
            for k in range(1, 4):
                c.op("dve", lambda k=k: V.scalar_tensor_tensor(out=xc[:], in0=xr[:, 1 + k:1 + k + TS], scalar=pp[:, k:k + 1], in1=xc[:], op0=ALU.mult, op1=ALU.add),
                     r=["b_xr", "b_pp", "b_xc"], w=["b_xc"])
            c.op("act", lambda: A.copy(out=xcb[:], in_=xc[:]), r=["b_xc"], w=["b_xcb"])
            for j in range(TS // 512):
                sl = slice(j * 512, (j + 1) * 512)
                pr, pi = pg[(2 * j) % 4], pg[(2 * j + 1) % 4]
                tr, ti = ("b_pg", (2 * j) % 4), ("b_pg", (2 * j + 1) % 4)
                c.op("pe", lambda pr=pr, sl=sl: T.matmul(pr[:], lhsT=wb[:, 0, :], rhs=xcb[:, sl], start=True, stop=True), r=["b_wb", "b_xcb"], w=[tr])
                c.op("pe", lambda pi=pi, sl=sl: T.matmul(pi[:], lhsT=wb[:, 1, :], rhs=xcb[:, sl], start=True, stop=True), r=["b_wb", "b_xcb"], w=[ti])
                c.op("act", lambda pr=pr, sl=sl: A.activation(out=rr[:, sl], in_=pr[:], func=AF.Sigmoid, bias=pp[:, 5:6], scale=1.0), r=[tr, "b_pp"], w=["b_r"])
                c.op("act", lambda pi=pi, sl=sl: A.activation(out=ii[:, sl], in_=pi[:], func=AF.Sigmoid, bias=pp[:, 6:7], scale=1.0), r=[ti, "b_pp"], w=["b_i"])
            c.op("act", lambda: A.activation(out=aa[:], in_=rr[:], func=AF.Exp, scale=nsp[:, 2:3]), r=["b_r", "b_nsp2"], w=["b_a"])
            c.op("act", lambda: A.activation(out=rr[:], in_=rr[:], func=AF.Exp, scale=nsp[:, 3:4]), r=["b_r", "b_nsp3"], w=["b_r"])
            c.op("act", lambda: A.activation(out=rr[:], in_=rr[:], func=AF.Sqrt, scale=-1.0, bias=e.epsc[:, 1:2]), r=["b_r", "epsc"], w=["b_r"])
            c.op("pool", lambda: G.tensor_tensor(out=ii[:], in0=ii[:], in1=xc[:], op=ALU.mult), r=["b_i", "b_xc"], w=["b_i"])
            c.op("dve", lambda: V.tensor_tensor(out=ii[:], in0=ii[:], in1=rr[:], op=ALU.mult), r=["b_i", "b_r"], w=["b_i"])
            ci = sg % 2
            c.op("dve", lambda ci=ci: V.tensor_tensor_scan(out=hs[:], data0=aa[:], data1=ii[:], initial=carry[:, ci:ci + 1], op0=ALU.mult, op1=ALU.add),
                 r=["b_a", "b_i", "b_carry"], w=["b_hs"])
            c.op("dve", lambda ci=ci: V.tensor_copy(out=carry[:, 1 - ci:2 - ci], in_=hs[:, TS - 1:TS]), r=["b_hs"], w=["b_carry"])
            c.op("act", lambda: A.activation(out=xg[:], in_=xg[:], func=AF.Gelu_apprx_tanh), r=["b_xg"], w=["b_xg"])
            c.op("dve", lambda: V.tensor_tensor(out=yb[:], in0=xg[:], in1=hs[:], op=ALU.mult), r=["b_xg", "b_hs"], w=["b_yb"])
            c.dma("act", e.mix_dst(hh, 1, 0, 128, t0, t0 + TS), yb[:], r=["b_yb"], w=[("mixT", hh, 1)], key="b_st")


def mixer_C(nc, c, l, hh, S, env):
    e = SimpleNamespace(**env)
    V, A, G, T = nc.vector, nc.scalar, nc.gpsimd, nc.tensor
    TS = min(2048, S)
    NSEG = S // TS
    NCH = S // 128
    cst = e.cst
    with ExitStack() as es:
        sb = lambda n, s, d: es.enter_context(nc.sbuf_tensor(_u(n), list(s), d))
        ps = lambda n, s, d=F32: es.enter_context(nc.psum_tensor(_u(n), list(s), d))
        pp = sb("c_pp", [128, NPAR], F32)
        raw = sb("c_raw", [128, TS + 4], F32)
        cv = sb("c_cv", [128, TS], F32)
        xsf = sb("c_xsf", [128, TS], F32)
        xsb = sb("c_xsb", [128, TS], BF16)
        Bb = sb("c_Bb", [128, TS], BF16)
        Cb = sb("c_Cb", [128, TS], BF16)
        sz = sb("c_sz", [128, TS], F32)
        dtcs = sb("c_dtcs", [34, TS], F32)
        adt = sb("c_adt", [34, TS], F32)
        onesT = sb("c_ones", [34, TS], F32)
        acol = sb("c_acol", [34, 2], F32)
        cscarry = sb("c_cscarry", [34, 2], F32)
        ncsend = sb("c_ncsend", [128, 2, NCH + 1], F32)
        xBt = [sb(f"c_xBt{i}", [128, 256], BF16) for i in range(2)]
        colsb = [sb(f"c_cols{i}", [128, 4], F32) for i in range(2)]
        ncs = [sb(f"c_ncs{i}", [128, 2], F32) for i in range(2)]
        tmpL = [sb(f"c_tmpL{i}", [128, 128], F32) for i in range(2)]
        LT = [sb(f"c_LT{i}", [128, 128], F32) for i in range(2)]
        MT = [sb(f"c_MT{i}", [128, 128], BF16) for i in range(2)]
        Er = [sb(f"c_Er{i}", [128, 128], F32) for i in range(2)]
        CsT = [sb(f"c_CsT{i}", [128, 128], BF16) for i in range(2)]
        xdt = [sb(f"c_xdt{i}", [128, 64], BF16) for i in range(2)]
        xdd = [sb(f"c_xdd{i}", [128, 64], BF16) for i in range(2)]
        state = [sb(f"c_state{i}", [128, 64], F32) for i in range(2)]
        prevT = [sb(f"c_prevT{i}", [128, 64], BF16) for i in range(2)]
        ysb = sb("c_ysb", [128, 512], F32)
        sq = sb("c_sq", [128, 512], F32)
        rs = sb("c_rs", [128, 512], F32)
        yob = [sb(f"c_yob{i}", [128, 512], BF16) for i in range(2)]
        rt = [ps(f"c_rt{i}", [128, 512]) for i in range(2)]
        yps = ps("c_yps", [128, 512])
        ssq = ps("c_ssq", [128, 512])
        gt = [ps(f"c_gt{i}", [128, 512]) for i in range(2)]
        trp = ps("c_trp", [128, 1024], BF16)
        misc = ps("c_misc", [128, 512])

        c.dma("sp", pp[:], e.pp_d.ap()[l, hh], w=["c_pp"], key="c_pp")
        c.op("pool", lambda: G.memset(dtcs[:], 0.0), w=["c_dtcs"])
        c.op("pool", lambda: G.memset(adt[:], 0.0), w=["c_adt"])
        c.op("pool", lambda: G.memset(onesT[:], 1.0), w=["c_ones"])
        c.op("pool", lambda: G.memset(cscarry[:], 0.0), w=["c_cscarry"])
        c.op("pool", lambda: G.memset(ncsend[:], 0.0), w=["c_ncsend"])
        for r in range(2):
            c.op("pool", lambda r=r: G.memset(state[r][:], 0.0), w=[("c_state", r)])
            c.op("pool", lambda r=r: G.memset(prevT[r][:], 0.0), w=[("c_prevT", r)])
        c.op("act", lambda: A.activation(out=acol[:, 0:1], in_=pp[0:34, 27:28], func=AF.Exp), r=["c_pp"], w=["c_acol0"])
        c.op("dve", lambda: V.tensor_scalar(out=acol[:, 1:2], in0=acol[:, 0:1], scalar1=-1.0, scalar2=None, op0=ALU.mult), r=["c_acol0"], w=["c_acol"])

        def conv_silu(src_d, wcol, dst_f, dst_b, t0, rtag):
            if t0 == 0:
                c.op("pool", lambda: G.memset(raw[:, 0:4], 0.0), w=["c_raw"])
                c.dma("sp", raw[:, 4:4 + TS], src_d[:, 0:TS], r=[rtag], w=["c_raw"], key="c_raw")
            else:
                c.dma("sp", raw[:, 1:4 + TS], src_d[:, t0 - 3:t0 + TS], r=[rtag], w=["c_raw"], key="c_raw")
            c.op("dve", lambda: V.tensor_scalar(out=cv[:], in0=raw[:, 1:1 + TS], scalar1=pp[:, wcol:wcol + 1], scalar2=pp[:, wcol + 4:wcol + 5], op0=ALU.mult, op1=ALU.add),
                 r=["c_raw", "c_pp"], w=["c_cv"])
            for k in range(1, 4):
                eng = "dve"
                EE = V
                c.op(eng, lambda k=k, EE=EE: EE.scalar_tensor_tensor(out=cv[:], in0=raw[:, 1 + k:1 + k + TS], scalar=pp[:, wcol + k:wcol + k + 1], in1=cv[:], op0=ALU.mult, op1=ALU.add),
                     r=["c_raw", "c_pp", "c_cv"], w=["c_cv"])
            if dst_f is not None:
                c.op("act", lambda: A.activation(out=dst_f[0][:], in_=cv[:], func=AF.Silu), r=["c_cv"], w=[dst_f[1]])
                c.op("dve", lambda: V.tensor_copy(out=dst_b[0][:], in_=dst_f[0][:]), r=[dst_f[1]], w=[dst_b[1]])
            else:
                c.op("act", lambda: A.activation(out=dst_b[0][:], in_=cv[:], func=AF.Silu), r=["c_cv"], w=[dst_b[1]])

        gchunk = 0
        for sg in range(NSEG):
            t0 = sg * TS
            conv_silu(e.ptf_d.ap()[hh, 3], 8, (xsf, "c_xsf"), (xsb, "c_xsb"), t0, ("ptf", hh, 3))
            conv_silu(e.ptf_d.ap()[hh, 4], 13, None, (Bb, "c_Bb"), t0, ("ptf", hh, 4))
            conv_silu(e.ptf_d.ap()[hh, 5], 18, None, (Cb, "c_Cb"), t0, ("ptf", hh, 5))
            c.dma("sp", sz[:], e.ptf_d.ap()[hh, 2][:, t0:t0 + TS], r=[("ptf", hh, 2)], w=["c_sz"], key="c_sz")
            c.op("act", lambda: A.activation(out=sz[:], in_=sz[:], func=AF.Silu), r=["c_sz"], w=["c_sz"])
            c.dma("sp", dtcs[0:2, :], e.dtT_d.ap()[hh, :, t0:t0 + TS], r=["dtT"], w=["c_dtcs"], key="c_dt")
            c.dma("sp", dtcs[32:34, :], e.dtT_d.ap()[hh, :, t0:t0 + TS], r=["dtT"], w=["c_dtcs"], key="c_dt")
            c.op("act", lambda: A.activation(out=dtcs[:], in_=dtcs[:], func=AF.Exp, bias=pp[0:34, 26:27], scale=1.0), r=["c_dtcs", "c_pp"], w=["c_dtcs"])
            c.op("act", lambda: A.activation(out=dtcs[:], in_=dtcs[:], func=AF.Ln, bias=e.epsc[0:34, 1:2], scale=1.0), r=["c_dtcs", "epsc"], w=["c_dtcs"])
            c.op("dve", lambda: V.tensor_scalar(out=adt[32:34, :], in0=dtcs[32:34, :], scalar1=acol[32:34, 1:2], scalar2=None, op0=ALU.mult),
                 r=["c_dtcs", "c_acol"], w=["c_adt"])
            ci = sg % 2
            c.op("dve", lambda ci=ci: V.tensor_tensor_scan(out=dtcs[32:34, :], data0=onesT[32:34, :], data1=adt[32:34, :], initial=cscarry[32:34, ci:ci + 1], op0=ALU.mult, op1=ALU.add),
                 r=["c_ones", "c_adt", "c_cscarry"], w=["c_dtcs"])
            c.op("dve", lambda ci=ci: V.tensor_copy(out=cscarry[32:34, 1 - ci:2 - ci], in_=dtcs[32:34, TS - 1:TS]), r=["c_dtcs"], w=["c_cscarry"])
            for g in range(TS // 512):
                gsl = slice(g * 512, (g + 1) * 512)
                for r in range(2):
                    c.op("pe", lambda r=r, gsl=gsl: T.matmul(rt[r][:], lhsT=cst[0:34, 4 + r, :], rhs=dtcs[:, gsl], start=True, stop=True),
                         r=["cst", "c_dtcs"], w=[("c_rt", r)])
                    c.op("dve", lambda r=r: V.tensor_scalar(out=ncsend[:, r, gchunk + 1:gchunk + 5], in0=rt[r][:, 127:512:128], scalar1=-1.0, scalar2=None, op0=ALU.mult),
                         r=[("c_rt", r)], w=["c_ncsend"])
                for ck in range(4):
                    k2 = gchunk % 2
                    sl = slice(g * 512 + ck * 128, g * 512 + (ck + 1) * 128)
                    csl = slice(ck * 128, (ck + 1) * 128)
                    c.op("pe", lambda sl=sl: T.transpose(trp[:, 0:128], xsb[:, sl], e.ident_b[:]), r=["c_xsb", "ident_b"], w=["c_trp"], sig=False)
                    c.op("pe", lambda sl=sl: T.transpose(trp[:, 128:256], Bb[:, sl], e.ident_b[:]), r=["c_Bb", "ident_b"], w=["c_trp"])
                    c.op("act", lambda k2=k2: A.copy(out=xBt[k2][:], in_=trp[:, 0:256]), r=["c_trp"], w=[("c_xBt", k2)])
                    c.op("pe", lambda sl=sl: T.matmul(misc[:, 128:132], lhsT=dtcs[:, sl], rhs=cst[0:34, 3, 0:4], start=True, stop=True),
                         r=["c_dtcs", "cst"], w=["c_misc_c"])
                    c.op("dve", lambda k2=k2: V.tensor_copy(out=colsb[k2][:], in_=misc[:, 128:132]), r=["c_misc_c"], w=[("c_cols", k2)])
                    c.op("dve", lambda k2=k2: V.tensor_scalar(out=ncs[k2][:], in0=colsb[k2][:, 2:4], scalar1=-1.0, scalar2=None, op0=ALU.mult),
                         r=[("c_cols", k2)], w=[("c_ncs", k2)])
                    c.op("pe", lambda k2=k2, sl=sl: T.matmul(gt[k2][:, 0:128], lhsT=Bb[:, sl], rhs=Cb[:, sl], start=True, stop=True),
                         r=["c_Bb", "c_Cb"], w=[("c_gt", k2)])
                    for r in range(2):
                        c.op("dve", lambda r=r, csl=csl: V.tensor_tensor(out=tmpL[r][:], in0=rt[r][:, csl], in1=cst[:, 1, :], op=ALU.add),
                             r=[("c_rt", r), "cst"], w=[("c_tmpL", r)])
                        c.op("act", lambda r=r, k2=k2: A.activation(out=LT[r][:], in_=tmpL[r][:], func=AF.Exp, bias=ncs[k2][:, r:r + 1], scale=1.0),
                             r=[("c_tmpL", r), ("c_ncs", k2)], w=[("c_LT", r)])
                        c.op("dve", lambda r=r, k2=k2: V.tensor_tensor(out=MT[r][:], in0=gt[k2][:, 0:128], in1=LT[r][:], op=ALU.mult),
                             r=[("c_gt", k2), ("c_LT", r)], w=[("c_MT", r)])
                        c.op("dve", lambda r=r, k2=k2: V.tensor_scalar(out=xdt[r][:], in0=xBt[k2][:, 64 * r:64 * r + 64], scalar1=colsb[k2][:, r:r + 1], scalar2=None, op0=ALU.mult),
                             r=[("c_xBt", k2), ("c_cols", k2)], w=[("c_xdt", r)])
                        c.op("dve", lambda r=r, k2=k2: V.tensor_scalar(out=xdd[r][:], in0=xBt[k2][:, 64 * r:64 * r + 64], scalar1=colsb[k2][:, r:r + 1], scalar2=LT[r][:, 127:128], op0=ALU.mult, op1=ALU.mult),
                             r=[("c_xBt", k2), ("c_cols", k2), ("c_LT", r)], w=[("c_xdd", r)])
                        c.op("act", lambda r=r, csl=csl: A.activation(out=Er[r][:], in_=rt[r][:, csl], func=AF.Exp, bias=ncsend[:, r, gchunk:gchunk + 1], scale=1.0),
                             r=[("c_rt", r), "c_ncsend"], w=[("c_Er", r)])
                        c.op("pool", lambda r=r, sl=sl: G.tensor_tensor(out=CsT[r][:], in0=Cb[:, sl], in1=Er[r][:], op=ALU.mult),
                             r=["c_Cb", ("c_Er", r)], w=[("c_CsT", r)])
                        c.op("pe", lambda r=r, csl=csl: T.matmul(yps[64 * r:64 * r + 64, csl], lhsT=xdt[r][:], rhs=MT[r][:], start=True, stop=False),
                             r=[("c_xdt", r), ("c_MT", r)], w=["c_yps"], sig=False)
                        c.op("pe", lambda r=r, csl=csl: T.matmul(yps[64 * r:64 * r + 64, csl], lhsT=prevT[r][:], rhs=CsT[r][:], start=False, stop=True),
                             r=[("c_prevT", r), ("c_CsT", r)], w=["c_yps"])
                        c.op("pe", lambda r=r, k2=k2: T.matmul(misc[:, 64 * r:64 * r + 64], lhsT=xBt[k2][:, 128:256], rhs=xdd[r][:], start=True, stop=True),
                             r=[("c_xBt", k2), ("c_xdd", r)], w=[("c_misc_s", r)])
                        c.op("dve", lambda r=r: V.scalar_tensor_tensor(out=state[r][:], in0=state[r][:], scalar=Er[r][:, 127:128], in1=misc[:, 64 * r:64 * r + 64], op0=ALU.mult, op1=ALU.add),
                             r=[("c_state", r), ("c_Er", r), ("c_misc_s", r)], w=[("c_state", r)])
                        c.op("act", lambda r=r: A.copy(out=prevT[r][:], in_=state[r][:]), r=[("c_state", r)], w=[("c_prevT", r)])
                    gchunk += 1
                gi = (sg * (TS // 512) + g) % 2
                c.op("dve", lambda gsl=gsl: V.scalar_tensor_tensor(out=ysb[:], in0=xsf[:, gsl], scalar=pp[:, 24:25], in1=yps[:], op0=ALU.mult, op1=ALU.add),
                     r=["c_xsf", "c_pp", "c_yps"], w=["c_ysb"])
                c.op("pool", lambda gsl=gsl: G.tensor_tensor(out=ysb[:], in0=ysb[:], in1=sz[:, gsl], op=ALU.mult), r=["c_ysb", "c_sz"], w=["c_ysb"])
                c.op("act", lambda: A.activation(out=sq[:], in_=ysb[:], func=AF.Square), r=["c_ysb"], w=["c_sq"])
                c.op("pe", lambda: T.matmul(ssq[:], lhsT=cst[:, 2, :], rhs=sq[:], start=True, stop=True), r=["cst", "c_sq"], w=["c_ssq"])
                c.op("act", lambda: A.activation(out=rs[:], in_=ssq[:], func=AF.Ln, scale=1.0 / 128, bias=e.epsc[:, 0:1]), r=["c_ssq", "epsc"], w=["c_rs"])
                c.op("act", lambda: A.activation(out=rs[:], in_=rs[:], func=AF.Exp, scale=-0.5), r=["c_rs"], w=["c_rs"])
                c.op("dve", lambda gi=gi: V.scalar_tensor_tensor(out=yob[gi][:], in0=ysb[:], scalar=pp[:, 23:24], in1=rs[:], op0=ALU.mult, op1=ALU.mult),
                     r=["c_ysb", "c_pp", "c_rs"], w=[("c_yob", gi)])
                tok0 = t0 + g * 512
                c.dma("act", e.mix_dst(hh, 2, 0, 128, tok0, tok0 + 512), yob[gi][:], r=[("c_yob", gi)], w=[("mixT", hh, 2)], key=("c_st", gi))


def mixer_D(nc, c, l, hh, S, lam_init, env):
    e = SimpleNamespace(**env)
    V, A, G, T = nc.vector, nc.scalar, nc.gpsimd, nc.tensor
    NB = S // 128
    NQ = S // 512
    scale = 32.0 ** -0.5
    cst = e.cst
    with ExitStack() as es:
        sb = lambda n, s, d: es.enter_context(nc.sbuf_tensor(_u(n), list(s), d))
        ps = lambda n, s, d=F32: es.enter_context(nc.psum_tensor(_u(n), list(s), d))
        pp = sb("d_pp", [128, NPAR], F32)
        kT = sb("d_kT", [128, S], BF16)
        qT = sb("d_qT", [128, S], BF16)
        Va = sb("d_Va", [128, NB, 2, 128], BF16)
        Rsb = [sb(f"d_R{h}", [128, WD], F32) for h in range(2)]
        c31 = sb("d_c31", [128, 2], F32)
        qm = [sb(f"d_qm{i}", [128, 4, 512], BF16) for i in range(2)]
        Et = [sb(f"d_E{i}", [128, 2, 512], BF16) for i in range(3)]
        tmp = [sb(f"d_tmp{i}", [128, 2, 512], F32) for i in range(2)]
        lamt = sb("d_lamt", [1, 160], F32)
        nlamc = sb("d_nlamc", [128, 2], F32)
        rc = [sb(f"d_rc{i}", [64, 512], F32) for i in range(2)]
        oo = [sb(f"d_o{i}", [64, 512], F32) for i in range(2)]
        od = sb("d_od", [64, 512], F32)
        sqd = sb("d_sqd", [64, 512], F32)
        rsd = sb("d_rsd", [64, 512], F32)
        yb = [sb(f"d_yb{i}", [64, 512], BF16) for i in range(2)]
        pall = ps("d_pall", [128, 8, 512])
        st = [pall[:, 0:2, :], pall[:, 2:4, :], pall[:, 4:6, :]]
        acc = [pall[:, 6, :], pall[:, 7, :]]
        ssq = pall[:, 0, :]
        pl = ssq

        c.dma("sp", pp[:], e.pp_d.ap()[l, hh], w=["d_pp"], key="d_pp")
        c.dma("sp", kT[:], e.ptb_d.ap()[hh, 3], r=[("ptb", hh, 3)], w=["d_kT"], key="d_kT")
        c.dma("sp", qT[:], e.ptb_d.ap()[hh, 2], r=[("ptb", hh, 2)], w=["d_qT"], key="d_qT")
        c.op("pool", lambda: G.memset(Va[:, :, :, 64:128], 1.0), w=["d_Va1"])
        for h in range(2):
            c.dma("sp", Va[:, :, h, 0:64], e.vd_d.ap()[hh, :, h * 64:(h + 1) * 64].rearrange("(n p) d -> p n d", p=128),
                  r=[("vd", hh)], w=["d_Va0"], key="d_Va")
            bi = hh * 2 + h
            c.dma("sp", Rsb[h][:], bass.AP(e.bandd_d, bi * 130 * WPD + 128, [[WPD, 128], [1, WD]]), r=[("bandd", bi)], w=[("d_R", h)], key="d_R")
            c.dma("sp", c31[:, h:h + 1], bass.AP(e.bvd_d, bi * LVD + LVD - 1, [[0, 128], [1, 1]]), w=["d_c31"], key="d_c31")
        for i in range(2):
            c.op("pool", lambda i=i: G.memset(qm[i][:], 0.0), w=[("d_qm", i)])
        c.dma("sp", lamt[0:1, 0:128], e.lam4_d.ap()[l:l + 1].rearrange("a f k -> a (f k)"), w=["d_lamt"], key="d_lamt")
        lv = lamt[0:1, 0:128].rearrange("p (a b k) -> p a b k", a=2, b=2)
        prod = sb("d_prod", [1, 2, 32], F32)
        red = sb("d_red", [1, 8], F32)
        c.op("dve", lambda: V.tensor_tensor(out=prod[:], in0=lv[:, :, 0, :], in1=lv[:, :, 1, :], op=ALU.mult), r=["d_lamt"], w=["d_prod"])
        c.op("dve", lambda: V.tensor_reduce(out=red[:, 0:2], in_=prod[:], op=ALU.add, axis=AX.X), r=["d_prod"], w=["d_red0"])
        c.op("act", lambda: A.activation(out=red[:, 2:4], in_=red[:, 0:2], func=AF.Exp), r=["d_red0"], w=["d_red1"])
        c.op("dve", lambda: V.tensor_tensor(out=red[:, 4:5], in0=red[:, 3:4], in1=red[:, 2:3], op=ALU.subtract), r=["d_red1"], w=["d_red2"])
        c.op("dve", lambda: V.tensor_scalar(out=red[:, 5:6], in0=red[:, 4:5], scalar1=-lam_init, scalar2=None, op0=ALU.add), r=["d_red2"], w=["d_red3"])
        c.op("dve", lambda: V.tensor_copy(out=red[:, 6:7], in_=red[:, 5:6]), r=["d_red3"], w=["d_red4"])
        c.op("pe", lambda: T.matmul(pl[:, 0:2], lhsT=cst[0:1, 2, :], rhs=red[0:1, 5:7], start=True, stop=True), r=["cst", "d_red4"], w=[("d_st", 0)])
        c.op("dve", lambda: V.tensor_copy(out=nlamc[:], in_=pl[:, 0:2]), r=[("d_st", 0)], w=["d_nlamc"])
        gcol = sb("d_gcol", [128, 1], F32)
        c.op("dve", lambda: V.tensor_scalar(out=gcol[:], in0=pp[:, 25:26], scalar1=1.0 - lam_init, scalar2=None, op0=ALU.mult), r=["d_pp"], w=["d_gcol"])

        nst = 0
        nE = 0
        ntmp = 0
        fin = 0
        LA = 2
        ssq = acc[0]
        SSQT = ("d_acc", 0)

        def qm_fill(Q):
            qmq = qm[Q % 2]
            for hc in range(4):
                if hc % 2 == 0:
                    c.op("act", lambda hc=hc: A.copy(out=qmq[32 * hc:32 * hc + 32, hc, :], in_=qT[32 * hc:32 * hc + 32, Q * 512:(Q + 1) * 512]), r=["d_qT"], w=[("d_qm", Q % 2)])
                else:
                    c.op("pool", lambda hc=hc: G.tensor_copy(out=qmq[32 * hc:32 * hc + 32, hc, :], in_=qT[32 * hc:32 * hc + 32, Q * 512:(Q + 1) * 512]), r=["d_qT"], w=[("d_qm", Q % 2)])

        def qk(Q, h, kb, si):
            c0 = max(0, kb - 4 * Q) * 128
            for cp in range(2):
                c.op("pe", lambda cp=cp: T.matmul(st[si][:, cp, c0:512], lhsT=kT[:, kb * 128:(kb + 1) * 128], rhs=qm[Q % 2][:, 2 * h + cp, c0:512], start=True, stop=True),
                     r=["d_kT", ("d_qm", Q % 2)], w=[("d_st", si)], sig=(cp == 1))

        def prologue(Q, h):
            for k in range(min(LA, 4 * Q + 4)):
                qk(Q, h, k, (nst + k) % 3)

        QH = [(Q, h) for Q in range(NQ) for h in range(2)]
        qm_fill(0)
        prologue(0, 0)
        for qi, (Q, h) in enumerate(QH):
            nkb = 4 * Q + 4
            if h == 1 and Q + 1 < NQ:
                qm_fill(Q + 1)
            for kb in range(nkb):
                si = nst % 3
                nst += 1
                if kb + LA < nkb:
                    qk(Q, h, kb + LA, (si + LA) % 3)
                ei = nE % 3
                nE += 1
                D = Q * 512 - kb * 128
                if D >= FAR_D:
                    c.op("act", lambda: A.activation(out=Et[ei][:], in_=st[si][:], func=AF.Exp, scale=scale, bias=c31[:, h:h + 1]),
                         r=[("d_st", si), "d_c31"], w=[("d_E", ei)])
                else:
                    ti = ntmp % 2
                    ntmp += 1
                    c0 = max(0, kb - 4 * Q) * 128
                    rb = bass.AP(Rsb[h], D + 384 + c0, [[WD, 128], [0, 2], [1, 512 - c0]])
                    c.op("dve", lambda: V.scalar_tensor_tensor(out=tmp[ti][:, :, c0:512], in0=st[si][:, :, c0:512], scalar=scale, in1=rb, op0=ALU.mult, op1=ALU.add),
                         r=[("d_st", si), ("d_R", h)], w=[("d_tmp", ti)])
                    c.op("act", lambda: A.activation(out=Et[ei][:, :, c0:512], in_=tmp[ti][:, :, c0:512], func=AF.Exp), r=[("d_tmp", ti)], w=[("d_E", ei)])
                c0 = max(0, kb - 4 * Q) * 128
                for cp in range(2):
                    c.op("pe", lambda cp=cp: T.matmul(acc[cp][:, c0:512], lhsT=Va[:, kb, h, :], rhs=Et[ei][:, cp, c0:512], start=(kb == 0), stop=(kb == nkb - 1)),
                         r=["d_Va0", "d_Va1", ("d_E", ei)], w=[("d_acc", cp)])
            for cp in range(2):
                c.op("act", lambda cp=cp: A.activation(out=rc[cp][:], in_=acc[cp][64:128, :], func=AF.Ln), r=[("d_acc", cp)], w=[("d_rc", cp)])
                c.op("act", lambda cp=cp: A.activation(out=rc[cp][:], in_=rc[cp][:], func=AF.Exp, scale=-1.0), r=[("d_rc", cp)], w=[("d_rc", cp)])
                c.op("dve", lambda cp=cp: V.tensor_tensor(out=oo[cp][:], in0=acc[cp][0:64, :], in1=rc[cp][:], op=ALU.mult), r=[("d_acc", cp), ("d_rc", cp)], w=[("d_o", cp)])
            if qi + 1 < len(QH):
                prologue(*QH[qi + 1])
            c.op("dve", lambda: V.scalar_tensor_tensor(out=od[:], in0=oo[1][:], scalar=nlamc[0:64, 0:1], in1=oo[0][:], op0=ALU.mult, op1=ALU.add),
                 r=[("d_o", 0), ("d_o", 1), "d_nlamc"], w=["d_od"])
            c.op("act", lambda: A.activation(out=sqd[:], in_=od[:], func=AF.Square), r=["d_od"], w=["d_sqd"])
            c.op("pe", lambda: T.matmul(ssq[0:64, :], lhsT=cst[0:64, 2, 0:64], rhs=sqd[:], start=True, stop=True), r=["cst", "d_sqd"], w=[SSQT])
            c.op("act", lambda: A.activation(out=rsd[:], in_=ssq[0:64, :], func=AF.Ln, scale=1.0 / 64, bias=e.epsc[0:64, 0:1]), r=[SSQT, "epsc"], w=["d_rsd"])
            c.op("act", lambda: A.activation(out=rsd[:], in_=rsd[:], func=AF.Exp, scale=-0.5), r=["d_rsd"], w=["d_rsd"])
            fi = fin % 2
            fin += 1
            c.op("dve", lambda: V.scalar_tensor_tensor(out=yb[fi][:], in0=od[:], scalar=gcol[0:64, 0:1], in1=rsd[:], op0=ALU.mult, op1=ALU.mult),
                 r=["d_od", "d_gcol", "d_rsd"], w=[("d_yb", fi)])
            c.dma("sp", e.mix_dst(hh, 3, 64 * h, 64, Q * 512, (Q + 1) * 512), yb[fi][:], r=[("d_yb", fi)], w=[("mixT", hh, 3)], key=("d_st", fi))


def mixer_A(nc, c, l, hh, S, env):
    e = SimpleNamespace(**env)
    V, A, G, T = nc.vector, nc.scalar, nc.gpsimd, nc.tensor
    NB = S // 128
    NSB = S // 2048
    scale = 64.0 ** -0.5
    with ExitStack() as es:
        sb = lambda n, s, d: es.enter_context(nc.sbuf_tensor(_u(n), list(s), d))
        ps = lambda n, s, d=F32: es.enter_context(nc.psum_tensor(_u(n), list(s), d))
        kT = sb("a_kT", [128, S], BF16)
        qT = sb("a_qT", [128, S], BF16)
        Va = [sb(f"a_Va{p}", [128, NB, 2, 128], BF16) for p in range(3)]
        Ra = [[sb(f"a_R{p}{h}", [128, WA], F32) for h in range(2)] for p in range(3)]
        qm = [sb(f"a_qm{h}", [128, 2048], BF16) for h in range(2)]
        Et = [sb(f"a_E{i}", [128, 256], BF16) for i in range(3)]
        tmp = [sb(f"a_tmp{i}", [128, 256], F32) for i in range(2)]
        rc = [sb(f"a_rc{i}", [64, 512], F32) for i in range(2)]
        yb = [sb(f"a_yb{i}", [64, 512], BF16) for i in range(2)]
        pall = ps("a_pall", [128, 8, 512])
        st = [pall[:, i, :] for i in range(3)]
        acc = [pall[:, 3 + i, :] for i in range(4)]

        c.dma("sp", kT[:], e.ptb_d.ap()[hh, 1], r=[("ptb", hh, 1)], w=["a_kT"], key="a_kT")
        c.dma("sp", qT[:], e.ptb_d.ap()[hh, 0], r=[("ptb", hh, 0)], w=["a_qT"], key="a_qT")
        for p, (_, dil) in enumerate(PATS):
            NBp = NB // dil
            c.op("pool", lambda p=p: G.memset(Va[p][:, :, :, 64:128], 1.0), w=[("a_Va1", p)])
            src = e.va_d.ap()[hh].rearrange("(n i r) (h d) -> r i n h d", i=128, r=dil, h=2)
            for r in range(dil):
                for h in range(2):
                    c.dma("sp", Va[p][:, r * NBp:(r + 1) * NBp, h, 0:64], src[r, :, :, h, :], r=[("va", hh)], w=[("a_Va0", p)], key="a_Va")
            for h in range(2):
                bi = (hh * 2 + h) * 3 + p
                c.dma("sp", Ra[p][h][:], bass.AP(e.banda_d, bi * 130 * WPA + 128, [[WPA, 128], [1, WA]]), r=[("banda", bi)], w=[("a_R", p, h)], key="a_R")
        for h in range(2):
            c.op("pool", lambda h=h: G.memset(qm[h][:], 0.0), w=[("a_qm", h)])

        nst = 0
        nE = 0
        ntmp = 0
        fin = 0
        LA = 2
        for sbk in range(NSB):
            T0 = sbk * 2048
            for h in range(2):
                c.op("act", lambda h=h: A.copy(out=qm[h][64 * h:64 * h + 64, :], in_=qT[64 * h:64 * h + 64, T0:T0 + 2048]), r=["a_qT"], w=[("a_qm", h)])
                started = [False] * 4
                steps = []
                for p, (_, dil) in enumerate(PATS):
                    nblk = 16 // dil
                    for r in range(dil):
                        for i in range(nblk):
                            steps.append((p, dil, r, i))

                def qk(step, si):
                    p, dil, r, i = step
                    nblk = 16 // dil
                    n = sbk * nblk + i
                    qs = slice(i * 128 * dil + r, (i + 1) * 128 * dil, dil)
                    ks = lambda nn: slice(nn * 128 * dil + r, (nn + 1) * 128 * dil, dil)
                    c.op("pe", lambda: T.matmul(st[si][:, 0:128], lhsT=kT[:, ks(n)], rhs=qm[h][:, qs], start=True, stop=True),
                         r=["a_kT", ("a_qm", h)], w=[("a_st", si)], sig=(n == 0))
                    if n >= 1:
                        c.op("pe", lambda: T.matmul(st[si][:, 128:256], lhsT=kT[:, ks(n - 1)], rhs=qm[h][:, qs], start=True, stop=True),
                             r=["a_kT", ("a_qm", h)], w=[("a_st", si)])

                for k in range(min(LA, len(steps))):
                    qk(steps[k], (nst + k) % 3)
                for idx, (p, dil, r, i) in enumerate(steps):
                    NBp = NB // dil
                    nblk = 16 // dil
                    n = sbk * nblk + i
                    si = nst % 3
                    nst += 1
                    if idx + LA < len(steps):
                        qk(steps[idx + LA], (si + LA) % 3)
                    W = 256 if n >= 1 else 128
                    ti = ntmp % 2
                    ntmp += 1
                    ei = nE % 3
                    nE += 1
                    c.op("dve", lambda: V.scalar_tensor_tensor(out=tmp[ti][:, 0:W], in0=st[si][:, 0:W], scalar=scale, in1=Ra[p][h][:, 0:W], op0=ALU.mult, op1=ALU.add),
                         r=[("a_st", si), ("a_R", p, h)], w=[("a_tmp", ti)])
                    c.op("act", lambda: A.activation(out=Et[ei][:, 0:W], in_=tmp[ti][:, 0:W], func=AF.Exp), r=[("a_tmp", ti)], w=[("a_E", ei)])
                    last = (p == 2 and r == dil - 1)
                    for part in range(2 if n >= 1 else 1):
                        nn = n - part
                        lhs = Va[p][:, r * NBp + nn, h, :]
                        if dil == 1:
                            segs = [((i * 128) // 512, slice((i % 4) * 128, (i % 4) * 128 + 128), slice(part * 128, part * 128 + 128))]
                        elif dil == 4:
                            segs = [(i, slice(r, 512, 4), slice(part * 128, part * 128 + 128))]
                        else:
                            segs = [(j, slice(r, 512, 16), slice(part * 128 + 32 * j, part * 128 + 32 * j + 32)) for j in range(4)]
                        for (bj, osl, esl) in segs:
                            stt = not started[bj]
                            started[bj] = True
                            c.op("pe", lambda: T.matmul(acc[bj][:, osl], lhsT=lhs, rhs=Et[ei][:, esl], start=stt, stop=last, skip_group_check=True),
                                 r=[("a_Va0", p), ("a_Va1", p), ("a_E", ei)], w=[("a_acc", bj)])
                for j in range(4):
                    fi = fin % 2
                    fin += 1
                    c.op("act", lambda: A.activation(out=rc[fi][:], in_=acc[j][64:128, :], func=AF.Ln), r=[("a_acc", j)], w=[("a_rc", fi)])
                    c.op("act", lambda: A.activation(out=rc[fi][:], in_=rc[fi][:], func=AF.Exp, scale=-1.0), r=[("a_rc", fi)], w=[("a_rc", fi)])
                    c.op("dve", lambda: V.tensor_tensor(out=yb[fi][:], in0=acc[j][0:64, :], in1=rc[fi][:], op=ALU.mult), r=[("a_acc", j), ("a_rc", fi)], w=[("a_yb", fi)])
                    c.dma("sp", e.mix_dst(hh, 0, 64 * h, 64, T0 + j * 512, T0 + (j + 1) * 512), yb[fi][:], r=[("a_yb", fi)], w=[("mixT", hh, 0)], key=("a_st", fi))


def phase_p3(nc, c, l, S, L, env):
    e = SimpleNamespace(**env)
    V, A, G, T = nc.vector, nc.scalar, nc.gpsimd, nc.tensor
    NQ = e.NQL
    MC = e.MC
    pair = e.pair
    last_layer = (l == L - 1)
    h_src = e.x_d if l == 0 else e.hbuf_d
    with ExitStack() as es:
        sb = lambda n, s, d: es.enter_context(nc.sbuf_tensor(_u(n), list(s), d))
        ps = lambda n, s, d=F32: es.enter_context(nc.psum_tensor(_u(n), list(s), d))
        wout = sb("f_wout", [128, MC, 1024], BF16)
        wdn = sb("f_wdn", [128, 32, 1024], BF16)
        gt = sb("f_gt", [128, 4, 1024], F32)
        mt = [sb(f"f_mt{i}", [128, MC, 512], BF16) for i in range(2 if pair else 1)]
        rsel = sb("f_rsel", [128, 2], F32)
        c.dma("sp", rsel[:], e.rsel_d.ap(), w=["f_rsel"], key="f_rsel")
        u2T = sb("f_u2T", [128, 8, 512], BF16)
        fT = sb("f_fT", [128, 32, 512], BF16)
        wupt = [sb(f"f_wup{i}", [128, 8, 256], BF16) for i in range(2)]
        ht = [sb(f"f_ht{i}", [128, 1024], F32) for i in range(1)]
        hmid = sb("f_hmid", [128, 4, 1024], F32)
        hnew = [sb(f"f_hnew{i}", [128, 1024], F32) for i in range(1)]
        ub = [sb(f"f_ub{i}", [128, 1024], BF16) for i in range(2)]
        sqv = [sb(f"f_sqv{i}", [128, 512], F32) for i in range(2)]
        ssq = sb("f_ssq", [128, 4], F32)
        ssp = sb("f_ssp", [128, 4], F32)
        uTs = [sb(f"f_uTs{i}", [128, 8, 512], BF16) for i in range(1)]
        po = [[ps(f"f_po{i}{k}", [128, 512]) for k in range(2)] for i in range(2)]
        pu = [ps(f"f_pu{i}", [128, 512]) for i in range(2)]
        pst = [ps(f"f_pst{i}", [128, 1024], BF16) for i in range(2)]

        c.dma("sp", wout[:], e.woutb_d.ap()[l].rearrange("(c p) d -> p c d", p=128), r=[("woutb", l)], w=["f_wout"], key="f_wout")
        c.dma("sp", wdn[:], e.wdnb_d.ap()[l].rearrange("(f p) d -> p f d", p=128), r=[("wdnb", l)], w=["f_wdn"], key="f_wdn")
        for k in range(1, 4):
            c.dma("sp", gt[:, k, :], bass.AP(e.gains_d, (l * 4 + k) * D_MODEL, [[0, 128], [1, D_MODEL]]), w=[("f_gt", k)], key="f_gt")
        if not last_layer:
            c.dma("sp", gt[:, 0, :], bass.AP(e.gains_d, ((l + 1) * 4) * D_MODEL, [[0, 128], [1, D_MODEL]]), w=[("f_gt", 0)], key="f_gt")

        def post_norm_res(pacc, ptags, gk, res_ap, res_tag, out_ap, out_tag):
            for k in range(2):
                c.op("act", lambda k=k: A.activation(out=sqv[k][:], in_=pacc[k][:], func=AF.Square, accum_out=ssp[:, k:k + 1]),
                     r=[ptags[k]], w=[("f_sqv", k), ("f_ssp", k)])
            c.op("dve", lambda: V.tensor_tensor(out=ssp[:, 2:3], in0=ssp[:, 0:1], in1=ssp[:, 1:2], op=ALU.add), r=[("f_ssp", 0), ("f_ssp", 1)], w=["f_ssp2"])
            rstd_from_ss_local(ssp[:, 2:3], ssp[:, 3:4], "f_ssp2", "f_ssp3")
            for k in range(2):
                c.op("dve", lambda k=k: V.scalar_tensor_tensor(out=out_ap[:, k * 512:(k + 1) * 512], in0=pacc[k][:], scalar=ssp[:, 3:4], in1=gt[:, gk, k * 512:(k + 1) * 512], op0=ALU.mult, op1=ALU.mult),
                     r=[ptags[k], "f_ssp3", ("f_gt", gk)], w=[out_tag])
            c.op("pool", lambda: G.tensor_tensor(out=out_ap, in0=out_ap, in1=res_ap, op=ALU.add), r=[out_tag, res_tag], w=[out_tag])

        tmpc = sb("f_tmpc", [128, 2], F32)

        def rstd_from_ss_local(ss_ap, out_ap, tin, tout):
            c.op("act", lambda: A.activation(out=tmpc[:, 0:1], in_=ss_ap, func=AF.Sqrt, scale=1.0 / D_MODEL, bias=e.epsc[:, 0:1]), r=[tin, "epsc"], w=["f_tmpc"])
            c.op("dve", lambda: V.reciprocal(out=out_ap, in_=tmpc[:, 0:1]), r=["f_tmpc"], w=[tout])

        def load_mix(q_):
            tk = q_ * 512
            m_ = mt[0]
            if not pair:
                c.dma("sp", m_[:], e.mixT_d[0].ap()[:, tk:tk + 512].rearrange("(c p) t -> p c t", p=128),
                      r=[("mixT", hh_, k_) for hh_ in range(e.NHH) for k_ in range(4)], w=[("f_mt", 0)], key=("f_mt", 0))
            else:
                for mm_ in range(4):
                    c.dma("sp", m_[:, mm_:8:4, :], e.mixg_d[mm_].ap()[:, tk:tk + 512].rearrange("(r p) t -> p r t", p=128), r=[("mixg", mm_)], w=[("f_mt", 0)], key=("f_mt", 0))
                    c.dma("sp", mt[1][:, mm_:8:4, :], e.mixg_d[mm_].ap()[:, e.SL + tk:e.SL + tk + 512].rearrange("(r p) t -> p r t", p=128), r=[("mixg", mm_)], w=[("f_mt", 1)], key=("f_mt", 1))

        nt = 0
        npo = 0
        npu = 0
        nw = 0
        for q in range(NQ):
            tok0 = q * 512
            m = mt[0]
            mtag = ("f_mt", 0)
            if q == 0:
                load_mix(0)
            pas = {}

            def outproj(j):
                nonlocal npo
                if pair:
                    js = slice(j * 128, (j + 1) * 128)
                    c.op("pool", lambda: G.tensor_scalar(out=m[:, :, js], in0=m[:, :, js], scalar1=rsel[:, 0:1], scalar2=0.0, op0=ALU.mult, op1=ALU.add), r=[mtag, "f_rsel"], w=[mtag])
                    c.op("dve", lambda: V.scalar_tensor_tensor(out=m[:, :, js], in0=mt[1][:, :, js], scalar=rsel[:, 1:2], in1=m[:, :, js], op0=ALU.mult, op1=ALU.add), r=[mtag, ("f_mt", 1), "f_rsel"], w=[mtag])
                pa = po[npo % 2]
                ptags = [("f_po", npo % 2, 0), ("f_po", npo % 2, 1)]
                npo += 1
                for k in range(2):
                    for cc in range(MC):
                        c.op("pe", lambda k=k, cc=cc: T.matmul(pa[k][:], lhsT=m[:, cc, j * 128:(j + 1) * 128], rhs=wout[:, cc, k * 512:(k + 1) * 512], start=(cc == 0), stop=(cc == MC - 1)),
                             r=[mtag, "f_wout"], w=[ptags[k]], sig=(cc == MC - 1))
                pas[j] = (pa, ptags)

            def post1(j):
                pa, ptags = pas[j]
                c.dma("sp", ht[0][:], h_src.ap()[tok0 + j * 128:tok0 + (j + 1) * 128, :], r=[("hbuf", q)] if l > 0 else [], w=[("f_ht", 0)], key=("f_ht", 0))
                post_norm_res(pa, ptags, 1, ht[0][:], ("f_ht", 0), hmid[:, j, :], ("f_hmid", j))
                e.norm_to_uT(hmid[:, j, :], ("f_hmid", j), gt[:, 2, :], ("f_gt", 2), ssq, ub, pst, u2T, "f_u2T", j, 0)

            outproj(0)
            for j in range(4):
                if j + 1 < 4:
                    outproj(j + 1)
                elif q + 1 < NQ:
                    load_mix(q + 1)
                post1(j)
            for fg in range(16):
                wt = wupt[nw % 2]
                wtag = ("f_wup", nw % 2)
                nw += 1
                c.dma("sp", wt[:], e.wupb_d.ap()[l][:, fg * 256:(fg + 1) * 256].rearrange("(c p) f -> p c f", p=128), r=[("wupb", l)], w=[wtag], key=wtag)
                for fc in range(2):
                    f = fg * 2 + fc
                    pi = npu % 2
                    npu += 1
                    for cc in range(8):
                        c.op("pe", lambda cc=cc: T.matmul(pu[pi][:], lhsT=wt[:, cc, fc * 128:(fc + 1) * 128], rhs=u2T[:, cc, :], start=(cc == 0), stop=(cc == 7)),
                             r=[wtag, "f_u2T"], w=[("f_pu", pi)], sig=(cc == 7))
                    c.op("act", lambda: A.activation(out=sqv[pi][:], in_=pu[pi][:], func=AF.Square), r=[("f_pu", pi)], w=[("f_sqv", pi)])
                    c.op("dve", lambda: V.scalar_tensor_tensor(out=fT[:, f, :], in0=pu[pi][:], scalar=0.0, in1=sqv[pi][:], op0=ALU.is_gt, op1=ALU.mult),
                         r=[("f_pu", pi), ("f_sqv", pi)], w=[("f_fT", f)])
            pbs = {}

            def down(j):
                nonlocal npo
                pa = po[npo % 2]
                ptags = [("f_po", npo % 2, 0), ("f_po", npo % 2, 1)]
                npo += 1
                for k in range(2):
                    for f in range(32):
                        c.op("pe", lambda k=k, f=f: T.matmul(pa[k][:], lhsT=fT[:, f, j * 128:(j + 1) * 128], rhs=wdn[:, f, k * 512:(k + 1) * 512], start=(f == 0), stop=(f == 31)),
                             r=[("f_fT", f), "f_wdn"], w=[ptags[k]], sig=(f == 31))
                pbs[j] = (pa, ptags)

            def post2(j):
                pa, ptags = pbs[j]
                hn = hnew[0]
                hntag = ("f_hnew", 0)
                post_norm_res(pa, ptags, 3, hmid[:, j, :], ("f_hmid", j), hn[:], hntag)
                dst = e.out_d if last_layer else e.hbuf_d
                c.dma("act", dst.ap()[tok0 + j * 128:tok0 + (j + 1) * 128, :], hn[:], r=[hntag], w=[("hbuf", q)] if not last_layer else [("outd", q)], key=("f_sth", j % 2))
                if not last_layer:
                    e.norm_to_uT(hn[:], hntag, gt[:, 0, :], ("f_gt", 0), ssq, ub, pst, uTs[0], ("f_uTs", 0), j, 1)

            down(0)
            for j in range(4):
                if j + 1 < 4:
                    down(j + 1)
                post2(j)
            if not last_layer:
                e.uT_store(q, uTs[0], ("f_uTs", 0), ("f_stu", 0))


def _consts():
    cst = np.zeros((128, 8, 128), np.float32)
    cst[:, 0, :] = np.eye(128, dtype=np.float32)
    s_ = np.arange(128)[:, None]
    l_ = np.arange(128)[None, :]
    cst[:, 1, :] = np.where(l_ >= s_, 0.0, NEG).astype(np.float32)
    cst[:, 2, :] = 1.0
    for i, r in enumerate((0, 1, 32, 33)):
        cst[r, 3, i] = 1.0
    cst[32, 4, :] = 1.0
    cst[33, 5, :] = 1.0
    return cst


def prep_core_inputs(inp, b, hhs, S=None, L=None, rank=0, tok_lo=0, tok_hi=None):
    f32 = np.float32
    L = L or inp["w_in"].shape[0]
    S = S or inp["x"].shape[1]
    NHH = len(hhs)
    cols_fm = []
    for hh in hhs:
        for base in (0, 256, 768, 1024, 1280, 1536, 1792, 2048, 2308, 2564):
            cols_fm.append(np.arange(base + 128 * hh, base + 128 * hh + 128))
    cols_tm = []
    for hh in hhs:
        cols_tm.append(np.arange(512 + 128 * hh, 512 + 128 * hh + 128))
        cols_tm.append(np.arange(2820 + 128 * hh, 2820 + 128 * hh + 128))
    cols_dt = [np.arange(2304 + 2 * hh, 2304 + 2 * hh + 2) for hh in hhs]
    cols = np.concatenate(cols_fm + cols_tm + cols_dt)
    w_in = np.ascontiguousarray(inp["w_in"][:L][:, :, cols]).astype(f32)
    rows = np.concatenate([np.arange(m * 256 + hh * 128, m * 256 + hh * 128 + 128) for hh in (0, 1) for m in range(4)])
    w_out = np.ascontiguousarray(inp["w_out"][:L][:, rows, :]).astype(f32)
    gains = np.stack([inp["norm_mix_pre"][:L], inp["norm_mix_post"][:L], inp["norm_ffn_pre"][:L], inp["norm_ffn_post"][:L]], axis=1).astype(f32)
    pp = np.zeros((L, NHH, 128, NPAR), f32)
    lruw = np.zeros((L, NHH, 2, 128, 128), f32)
    for i, hh in enumerate(hhs):
        ch = slice(128 * hh, 128 * hh + 128)
        pp[:, i, :, 0:4] = np.transpose(inp["lru_conv_w"][:L][:, :, ch], (0, 2, 1))
        pp[:, i, :, 4] = inp["lru_conv_b"][:L][:, ch]
        pp[:, i, :, 5] = inp["lru_ba"][:L][:, ch]
        pp[:, i, :, 6] = inp["lru_bx"][:L][:, ch]
        pp[:, i, :, 7] = inp["lru_lambda"][:L][:, ch]
        for k, off in enumerate((0, 256, 512)):
            cs_ = slice(off + 128 * hh, off + 128 * hh + 128)
            pp[:, i, :, 8 + 5 * k:12 + 5 * k] = np.transpose(inp["ssm_conv_w"][:L][:, :, cs_], (0, 2, 1))
            pp[:, i, :, 12 + 5 * k] = inp["ssm_conv_b"][:L][:, cs_]
        pp[:, i, :, 23] = inp["ssm_norm"][:L][:, ch]
        pp[:, i, :, 24] = np.repeat(inp["ssm_d"][:L][:, 2 * hh:2 * hh + 2], 64, axis=1)
        pp[:, i, :, 25] = np.tile(inp["diff_norm"][:L], (1, 2))
        for r in range(2):
            for row in (r, 32 + r):
                pp[:, i, row, 26] = inp["ssm_dt_bias"][:L][:, 2 * hh + r]
                pp[:, i, row, 27] = inp["ssm_a_log"][:L][:, 2 * hh + r]
        for k, nm in enumerate(("lru_wa", "lru_wx")):
            for r in range(2):
                lruw[:, i, k, 64 * r:64 * r + 64, 64 * r:64 * r + 64] = inp[nm][:L][:, 2 * hh + r]
    lam4 = np.stack([inp["diff_lq1"][:L], inp["diff_lk1"][:L], inp["diff_lq2"][:L], inp["diff_lk2"][:L]], axis=1).astype(f32)
    rel = np.asarray(inp["rel_bias"]).astype(f32)
    dist = np.arange(LVD) - 511
    bidx = t5_bucket_np(dist)
    bvd = np.zeros((NHH * 2, LVD), f32)
    bva = np.zeros((NHH * 2 * 3, LVA), f32)
    relv = np.arange(LVA) - 127
    for i, hh in enumerate(hhs):
        for h in range(2):
            head = 2 * hh + h
            bvd[i * 2 + h] = np.where(dist >= 0, rel[bidx, 4 + head], f32(NEG))
            for p, (_, dil) in enumerate(PATS):
                ok = (relv >= 0) & (relv <= 128)
                bva[(i * 2 + h) * 3 + p] = np.where(ok, rel[t5_bucket_np(relv * dil), head], f32(NEG))
    rsel = np.zeros((128, 2), f32)
    rsel[:, rank] = 1.0
    return {
        "x": np.ascontiguousarray(inp["x"][b, :S][tok_lo:tok_hi]).astype(f32), "rsel": rsel,
        "w_in": w_in, "w_out": w_out,
        "w_up": np.ascontiguousarray(inp["w_ff_up"][:L]).astype(f32),
        "w_dn": np.ascontiguousarray(inp["w_ff_down"][:L]).astype(f32),
        "gains": gains, "pp": pp, "lruw": lruw, "lam4": lam4, "bvd": bvd, "bva": bva, "cst": _consts(),
    }


_NC_CACHE = {}


def kernel(**inputs):
    inp = {k: np.asarray(v) for k, v in inputs.items()}
    B, S, _ = inp["x"].shape
    L = inp["w_in"].shape[0]
    key = (S, L)
    if key not in _NC_CACHE:
        _NC_CACHE[key] = build(S=S, NHH=1, L=L, pair=True)[0]
    nc = _NC_CACHE[key]
    H = S // 2
    in_maps = [prep_core_inputs(inp, i // 2, [i % 2], rank=i % 2, tok_lo=(i % 2) * H, tok_hi=(i % 2 + 1) * H) for i in range(8)]
    res = run_bass_kernel_spmd(nc, in_maps, core_ids=list(range(8)))
    out = np.empty((B, S, D_MODEL), np.float32)
    for i in range(8):
        out[i // 2, (i % 2) * H:(i % 2 + 1) * H] = res.results[i]["out"]
    return out
```
